# Optimizing a Trainium2 kernel written in Bass

```python
import jax, jax.numpy as jnp
from jax import lax
import numpy as np

D_MODEL = 1024
BATCH = 8
SEQ = 2048
DEPTH = 2

GRID_W = 64
CTX_LEN = 256
HEAD_DIM = 64
N_Q_HEADS = 16
N_KV_HEADS = 4
Q_PER_KV = N_Q_HEADS // N_KV_HEADS
ATTN_WIDTH = N_Q_HEADS * HEAD_DIM
KV_WIDTH = N_KV_HEADS * HEAD_DIM
CONV_WIDTH = D_MODEL
CONV_K = 3
N_BRANCH = 2
Q_BLOCK = 128
ROPE_THETA = 10000.0
EPS = 1e-6
ATTN_SCALE = HEAD_DIM ** -0.5
PROJ_SIZES = (ATTN_WIDTH, KV_WIDTH, KV_WIDTH, ATTN_WIDTH, CONV_WIDTH, CONV_WIDTH, CONV_WIDTH, CONV_WIDTH, N_BRANCH * D_MODEL)
PROJ_WIDTH = 2 * ATTN_WIDTH + 2 * KV_WIDTH + 4 * CONV_WIDTH + N_BRANCH * D_MODEL

kernel_name = "hybrid_gqa_shortconv_dit_prefix"


def rms_norm(x, g):
    x32 = x.astype(jnp.float32)
    y = x32 * lax.rsqrt(jnp.mean(x32 * x32, axis=-1, keepdims=True) + EPS)
    return y.astype(x.dtype) * g


def modulation(cond, w_ada, b_ada):
    m = jax.nn.silu(cond) @ w_ada + b_ada
    return jnp.split(m, 3, axis=-1)


def modulate(x, g, shift, scale):
    return rms_norm(x, g) * (1.0 + scale) + shift


def split_proj(p):
    points = []
    acc = 0
    for s in PROJ_SIZES[:-1]:
        acc += s
        points.append(acc)
    return jnp.split(p, points, axis=-1)


def axial_rope(n_tokens, dtype):
    rows = n_tokens // GRID_W
    row = jnp.repeat(jnp.arange(rows), GRID_W).astype(jnp.float32)
    col = jnp.tile(jnp.arange(GRID_W), rows).astype(jnp.float32)
    half = HEAD_DIM // 2
    inv = ROPE_THETA ** (-jnp.arange(0, half, 2, dtype=jnp.float32) / half)
    ang_r = row[:, None] * inv[None, :]
    ang_c = col[:, None] * inv[None, :]
    ang = jnp.concatenate([ang_r, ang_r, ang_c, ang_c], axis=-1)
    return jnp.cos(ang)[:, None, :].astype(dtype), jnp.sin(ang)[:, None, :].astype(dtype)


def rotate_axial(x):
    x1, x2, x3, x4 = jnp.split(x, 4, axis=-1)
    return jnp.concatenate([-x2, x1, -x4, x3], axis=-1)


def apply_rope(x, cos, sin):
    return x * cos + rotate_axial(x) * sin


def gqa(q, k, v):
    b, lq = q.shape[:2]
    qg = q.reshape(b, lq, N_KV_HEADS, Q_PER_KV, HEAD_DIM)
    s = jnp.einsum('bqkgd,btkd->bkgqt', qg, k) * ATTN_SCALE
    p = jax.nn.softmax(s.astype(jnp.float32), axis=-1).astype(v.dtype)
    o = jnp.einsum('bkgqt,btkd->bqkgd', p, v)
    return o.reshape(b, lq, ATTN_WIDTH)


def latent_attention(q, k_all, v_all):
    b, s = q.shape[:2]
    nb = s // Q_BLOCK
    qb = q.reshape(b, nb, Q_BLOCK, N_Q_HEADS, HEAD_DIM).transpose(1, 0, 2, 3, 4)
    o = lax.map(lambda blk: gqa(blk, k_all, v_all), qb)
    return o.transpose(1, 0, 2, 3).reshape(b, s, ATTN_WIDTH)


def project(h, w_in, qn_g, kn_g):
    b, l = h.shape[:2]
    q, k, v, za, xc, bc, cc, zc, gl = split_proj(h @ w_in)
    q = rms_norm(q.reshape(b, l, N_Q_HEADS, HEAD_DIM), qn_g)
    k = rms_norm(k.reshape(b, l, N_KV_HEADS, HEAD_DIM), kn_g)
    v = v.reshape(b, l, N_KV_HEADS, HEAD_DIM)
    return q, k, v, za, xc, bc, cc, zc, gl


def context_kv(h, w_in, kn_g):
    b, l = h.shape[:2]
    kv = h @ w_in[:, ATTN_WIDTH:ATTN_WIDTH + 2 * KV_WIDTH]
    k, v = jnp.split(kv, 2, axis=-1)
    k = rms_norm(k.reshape(b, l, N_KV_HEADS, HEAD_DIM), kn_g)
    return k, v.reshape(b, l, N_KV_HEADS, HEAD_DIM)


def short_conv(u, w, bias):
    y = lax.conv_general_dilated(u, w[:, None, :].astype(u.dtype), window_strides=(1,), padding=((1, 1),),
                                 dimension_numbers=('NWC', 'WIO', 'NWC'), feature_group_count=CONV_WIDTH)
    return y + bias


def merge_output(attn_o, za, xc, bc, cc, zc, gl, conv_w, conv_b, w_attn_br, w_conv_br, b_gate, w_out):
    attn_br = (attn_o * jax.nn.silu(za)) @ w_attn_br
    y = bc * short_conv(cc * xc, conv_w, conv_b)
    conv_br = (y * jax.nn.silu(zc)) @ w_conv_br
    gates = jax.nn.sigmoid(gl.reshape(*gl.shape[:-1], N_BRANCH, D_MODEL) + b_gate)
    merged = gates[..., 0, :] * attn_br + gates[..., 1, :] * conv_br
    return merged @ w_out


def setup_inputs(seed: int = 0) -> dict:
    key = jax.random.key(seed)
    ks = jax.random.split(key, 16)
    f = jnp.float32
    n = lambda k, shape: jax.random.normal(k, shape, dtype=f)
    return {
        "x": n(ks[0], (BATCH, SEQ, D_MODEL)),
        "c": n(ks[1], (BATCH, D_MODEL)),
        "ctx": n(ks[2], (BATCH, CTX_LEN, D_MODEL)),
        "c_ctx": n(ks[3], (D_MODEL,)),
        "norm_g": 1.0 + 0.02 * n(ks[4], (DEPTH, D_MODEL)),
        "w_ada": n(ks[5], (DEPTH, D_MODEL, 3 * D_MODEL)) * (0.5 * D_MODEL ** -0.5),
        "b_ada": 0.02 * n(ks[6], (DEPTH, 3 * D_MODEL)),
        "w_in": n(ks[7], (DEPTH, D_MODEL, PROJ_WIDTH)) * D_MODEL ** -0.5,
        "q_norm_g": 1.0 + 0.02 * n(ks[8], (DEPTH, HEAD_DIM)),
        "k_norm_g": 1.0 + 0.02 * n(ks[9], (DEPTH, HEAD_DIM)),
        "conv_w": n(ks[10], (DEPTH, CONV_K, CONV_WIDTH)) * CONV_K ** -0.5,
        "conv_b": 0.02 * n(ks[11], (DEPTH, CONV_WIDTH)),
        "w_attn_br": n(ks[12], (DEPTH, ATTN_WIDTH, D_MODEL)) * ATTN_WIDTH ** -0.5,
        "w_conv_br": n(ks[13], (DEPTH, CONV_WIDTH, D_MODEL)) * CONV_WIDTH ** -0.5,
        "b_gate": 0.02 * n(ks[14], (DEPTH, N_BRANCH, D_MODEL)),
        "w_out": n(ks[15], (DEPTH, D_MODEL, D_MODEL)) * D_MODEL ** -0.5,
    }


def reference(x, c, ctx, c_ctx, norm_g, w_ada, b_ada, w_in, q_norm_g, k_norm_g, conv_w, conv_b,
              w_attn_br, w_conv_br, b_gate, w_out):
    s = x.shape[1]
    cos, sin = axial_rope(s, x.dtype)
    for l in range(DEPTH):
        last = l == DEPTH - 1
        sh_x, sc_x, g_x = [m[:, None, :] for m in modulation(c, w_ada[l], b_ada[l])]
        sh_c, sc_c, g_c = modulation(c_ctx, w_ada[l], b_ada[l])
        hx = modulate(x, norm_g[l], sh_x, sc_x)
        hc = modulate(ctx, norm_g[l], sh_c, sc_c)

        qx, kx, vx, za, xc, bc, cc, zc, gl = project(hx, w_in[l], q_norm_g[l], k_norm_g[l])
        qx = apply_rope(qx, cos, sin)
        kx = apply_rope(kx, cos, sin)
        if last:
            kc, vc = context_kv(hc, w_in[l], k_norm_g[l])
        else:
            qc, kc, vc, za_c, xc_c, bc_c, cc_c, zc_c, gl_c = project(hc, w_in[l], q_norm_g[l], k_norm_g[l])

        k_all = jnp.concatenate([kc, kx], axis=1)
        v_all = jnp.concatenate([vc, vx], axis=1)
        attn_x = latent_attention(qx, k_all, v_all)
        out_x = merge_output(attn_x, za, xc, bc, cc, zc, gl, conv_w[l], conv_b[l],
                             w_attn_br[l], w_conv_br[l], b_gate[l], w_out[l])
        if not last:
            attn_c = gqa(qc, kc, vc)
            out_c = merge_output(attn_c, za_c, xc_c, bc_c, cc_c, zc_c, gl_c, conv_w[l], conv_b[l],
                                 w_attn_br[l], w_conv_br[l], b_gate[l], w_out[l])
            ctx = ctx + g_c * out_c
        x = x + g_x * out_x
    return x
```

```python
import numpy as np
import concourse.bass as bass
import concourse.mybir as mybir
from concourse.bass_utils import run_bass_kernel_spmd

F32 = mybir.dt.float32
BF16 = mybir.dt.bfloat16
ALU = mybir.AluOpType
AF = mybir.ActivationFunctionType
AX = mybir.AxisListType

D = 1024
S = 2048
CTX = 256
T = S + CTX
DEPTH = 2
HD = 64
NH = 16
NKV = 4
KC = 8
NBLK = 23
EPS = 1e-6
G = 512
NSLOT = 3
NSCR = 4
N_CORES = 8


class Buf:
    __slots__ = ("name", "w", "r")

    def __init__(self, name):
        self.name = name
        self.w = None
        self.r = {}


class Sched:
    ENGS = ("pe", "act", "dve", "pool", "sp")

    def __init__(self, nc):
        self.nc = nc
        self.streams = {e: [] for e in self.ENGS}
        self.esem = {e: nc.alloc_semaphore("es_" + e) for e in ("pe", "act", "dve", "pool")}
        self.ecnt = {e: 0 for e in self.esem}
        self.waited = {e: {} for e in self.ENGS}
        self.dsem = {}
        self.dcnt = {}
        self.nops = 0
        self.npe = 0
        self.marks = []

    def _deps(self, e, reads, writes):
        need = {}

        def add(tok):
            if tok is None:
                return
            s, v = tok
            if need.get(s, 0) < v:
                need[s] = v

        for b in reads:
            add(b.w)
        for b in writes:
            add(b.w)
            for s, v in b.r.items():
                add((s, v))
        own = self.esem.get(e)
        out = []
        wd = self.waited[e]
        for s, v in need.items():
            if e == "pe" and s is own:
                continue
            if wd.get(s, 0) < v:
                wd[s] = v
                out.append((s, v))
        return out

    def _mark(self, tok, reads, writes):
        s, v = tok
        for b in reads:
            if b.r.get(s, 0) < v:
                b.r[s] = v
        for b in writes:
            b.w = tok
            b.r = {}

    def mark(self, name):
        self.marks.append((name, self.npe))

    def op(self, e, fns, reads=(), writes=()):
        if e == "pe":
            self.npe += len(fns)
        waits = self._deps(e, reads, writes)
        sem = self.esem[e]
        self.ecnt[e] += 1
        tok = (sem, self.ecnt[e])

        def emit(eng, waits=waits, fns=fns, sem=sem):
            for s, v in waits:
                eng.wait_ge(s, v)
            for f in fns[:-1]:
                f(eng)
            fns[-1](eng).then_inc(sem, 1)

        self.streams[e].append(emit)
        self._mark(tok, reads, writes)
        self.nops += 1
        return tok

    def dma(self, q, key, fn, reads=(), writes=()):
        if key not in self.dsem:
            self.dsem[key] = self.nc.alloc_semaphore("ds_" + key)
            self.dcnt[key] = 0
        waits = self._deps(q, reads, writes)
        sem = self.dsem[key]
        self.dcnt[key] += 16
        tok = (sem, self.dcnt[key])

        def emit(eng, waits=waits, fn=fn, sem=sem):
            for s, v in waits:
                eng.wait_ge(s, v)
            fn(eng).then_inc(sem, 16)

        self.streams[q].append(emit)
        self._mark(tok, reads, writes)
        return tok

    def wait_all(self, q, toks):
        def emit(eng, toks=toks):
            for s, v in toks:
                eng.wait_ge(s, v)

        self.streams[q].append(emit)

    def replay(self, block):
        st = self.streams

        @block.sync
        def _(e):
            for f in st["sp"]:
                f(e)

        @block.gpsimd
        def _(e):
            for f in st["pool"]:
                f(e)

        @block.tensor
        def _(e):
            for f in st["pe"]:
                f(e)

        @block.vector
        def _(e):
            for f in st["dve"]:
                f(e)

        @block.scalar
        def _(e):
            for f in st["act"]:
                f(e)


class Ring:
    def __init__(self, items):
        self.items = items
        self.i = 0

    def next(self):
        it = self.items[self.i % len(self.items)]
        self.i += 1
        return it


def build_program(layers, stop=None):
    L = len(layers)
    nc = bass.Bass("TRN2", target_bir_lowering=False)
    sch = Sched(nc)

    def dram(name, shape, dt=F32, kind="ExternalInput"):
        return nc.dram_tensor(name, list(shape), dt, kind=kind).ap()

    d_xT = dram("xT", [D, T])
    d_cvec = dram("cvec", [128, KC * 2])
    d_wada = dram("wada", [L, D, 3 * D])
    d_bada = dram("bada", [128, L * 24])
    d_ng = dram("ng", [128, L * 8])
    d_wcat = dram("wcat", [L, NBLK, D, 512])
    d_qg = dram("qg", [128, L * 64])
    d_kg = dram("kg", [128, L * 64])
    d_qgp = dram("qgp", [128, L * 64])
    d_kgp = dram("kgp", [128, L * 64])
    d_convw = dram("convw", [128, L * 3 * 8])
    d_convb = dram("convb", [128, L * 8])
    d_bgate = dram("bgate", [128, L * 2 * 8])
    d_cos = dram("cosT", [128, 16 * 64])
    d_sin = dram("sinT", [128, 16 * 64])
    d_ident = dram("ident", [128, 128])
    d_out = dram("outT", [D, T], kind="ExternalOutput")
    d_wbf = nc.dram_tensor("wbf", [L, NBLK, 128, KC * 512], BF16, kind="Internal").ap()

    def sb(name, shape, dt=F32):
        return nc.alloc_sbuf_tensor(name, list(shape), dt)

    xT = sb("xT_sb", [128, KC, T])
    rstd_all = sb("rstd_all", [128, T])
    kT = sb("kT_sb", [128, 4, T], BF16)
    Vt = sb("Vt_sb", [128, 18, 384], BF16)
    hT_raw = [sb("hT%d" % i, [128, KC * (G + 1) + 0], BF16) for i in range(2)]
    hT = [h[:, :].rearrange("p (kc t) -> p kc t", kc=KC) for h in hT_raw]
    hT_f = [h.bitcast(F32) for h in hT_raw]
    gbuf = [sb("gbuf%d" % i, [128, KC, G], BF16) for i in range(3)]
    wsl = [sb("wslot%d" % i, [128, KC, 512], BF16) for i in range(NSLOT)]
    wsl_f = [w.bitcast(F32) for w in wsl]
    scr = [sb("scr%d" % i, [128, 1024]) for i in range(NSCR)]
    small = [sb("small%d" % i, [128, 16]) for i in range(8)]
    uprev = sb("uprev", [128, 16])
    rd_t = sb("rd_t", [128, 512])
    c_cvec = sb("c_cvec", [128, KC * 2])
    c_sc = sb("c_sc", [128, KC * 2])
    c_bada = sb("c_bada", [128, L * 24])
    c_ng = sb("c_ng", [128, L * 8])
    c_qg = sb("c_qg", [128, L * 64])
    c_kg = sb("c_kg", [128, L * 64])
    c_qgp = sb("c_qgp", [128, L * 64])
    c_kgp = sb("c_kgp", [128, L * 64])
    cgsg = [sb("cgsg%d" % i, [128, 128]) for i in range(1)]
    c_convw = sb("c_convw", [128, L * 24])
    c_convb = sb("c_convb", [128, L * 8])
    c_bgate = sb("c_bgate", [128, L * 16])
    c_cos = sb("c_cos", [128, 16 * 64])
    c_sin = sb("c_sin", [128, 16 * 64])
    c_ident = sb("c_ident", [128, 128], BF16)
    c_ones = sb("c_ones", [128, 128])
    c_eps = sb("c_eps", [128, 1])
    c_mod = sb("c_mod", [128, L * 48])
    c_gs = sb("c_gs", [128, L * 16])
    c_negB = sb("c_negB", [128, L])
    psS = [nc.alloc_psum_tensor("psS%d" % i, [128, 1024], F32) for i in range(2)]
    ps1 = [nc.alloc_psum_tensor("ps%d" % i, [128, 512], F32) for i in range(4, 8)]
    ps = [psS[0][:, 0:512], psS[0][:, 512:1024], psS[1][:, 0:512], psS[1][:, 512:1024]] + [p[:, :] for p in ps1]
    ps_bf = {7: ps1[3].bitcast(BF16)}

    B = {}

    def buf(name):
        if name not in B:
            B[name] = Buf(name)
        return B[name]

    groups = [(0, CTX, True)] + [(CTX + G * j, G, False) for j in range(S // G)]
    NG = len(groups)
    GORDER = list(range(1, NG)) + [0]

    def xb(gi, kc):
        return buf("x_%d_%d" % (gi, kc))

    def xbs(gi):
        return [xb(gi, kc) for kc in range(KC)]

    b_rstd = [buf("rstd_%d" % gi) for gi in range(NG)]
    b_kT = [buf("kT_%d" % kt) for kt in range(18)]
    b_V = [buf("V_%d" % kt) for kt in range(18)]
    b_hT = [[buf("hT_%d_%d" % (i, kc)) for kc in range(KC)] for i in range(2)]
    b_g = [[buf("g_%d_%d" % (i, kc)) for kc in range(KC)] for i in range(3)]
    b_ws = [buf("ws_%d" % i) for i in range(NSLOT)]
    b_scr = [buf("scr_%d" % i) for i in range(NSCR)]
    b_small = [buf("small_%d" % i) for i in range(8)]
    b_uprev = [buf("uprev_%d" % c) for c in range(8)]
    b_rd = buf("rd_t")
    b_cgsg = [buf("cgsg_%d" % i) for i in range(1)]
    cgsg_ring = Ring([0])
    b_ps = [buf("ps_%d" % i) for i in range(8)]
    b_const = buf("const")
    b_ident = buf("ident")
    b_sc = buf("sc")
    b_mod = [buf("mod_%d" % l) for l in range(L)]
    b_negB = [buf("negB_%d" % l) for l in range(L)]
    b_misc = buf("misc_consts")
    b_out = buf("out")

    scr_ring = Ring(list(range(NSCR)))
    small_ring = Ring(list(range(8)))

    def scratch():
        i = scr_ring.next()
        return scr[i], b_scr[i]

    def smalls():
        i = small_ring.next()
        return small[i], b_small[i]

    consts = [(c_cvec, d_cvec), (c_bada, d_bada), (c_ng, d_ng), (c_qg, d_qg), (c_kg, d_kg), (c_qgp, d_qgp), (c_kgp, d_kgp),
              (c_convw, d_convw), (c_convb, d_convb), (c_bgate, d_bgate), (c_cos, d_cos),
              (c_sin, d_sin), (scr[0][:, 0:128], d_ident)]
    ctok = None
    for t_sb, t_dr in consts:
        ctok = sch.dma("sp", "const", (lambda e, o=t_sb, i=t_dr: e.dma_start(out=o[:, :], in_=i)), writes=[b_const, b_scr[0]])

    xT_dr = d_xT.rearrange("(kc p) t -> p kc t", p=128)
    for gi, (t0, n, _) in enumerate(groups):
        sch.dma("sp", "xld%d" % gi,
                (lambda e, t0=t0, n=n: e.dma_start(out=xT[:, :, t0:t0 + n], in_=xT_dr[:, :, t0:t0 + n])),
                writes=xbs(gi))

    sch.op("dve", [lambda e: e.memset(c_ones[:, :], 1.0)], writes=[b_misc])
    sch.op("dve", [lambda e: e.memset(c_eps[:, :], EPS)], writes=[b_misc])
    Vt5 = Vt[:, :, :].rearrange("p k (m b d) -> p k m b d", m=2, b=3, d=64)
    for kt in range(18):
        sch.op("dve", [lambda e, kt=kt: e.memset(Vt5[:, kt, :, 1, :], 1.0)], writes=[b_V[kt]])
    for g_ in range(4):
        sch.op("dve", [lambda e, g_=g_: e.memset(kT[:, g_, :], 0.0)], writes=b_kT)
    sch.op("dve", [lambda e: e.tensor_copy(out=c_ident[:, :], in_=scr[0][:, 0:128])], reads=[b_const, b_scr[0]], writes=[b_ident])
    sch.op("act", [lambda e: e.activation(out=c_sc[:, :], in_=c_cvec[:, :], func=AF.Silu)], reads=[b_const], writes=[b_sc])

    def group_take_order(l, gi):
        isctx = groups[gi][2]
        seq = [("cat", l, 1), ("cat", l, 2)]
        ada = {}
        if l + 1 < L and isctx:
            ada = {hk: hk - 1 for hk in range(1, 13)}
        for hk in range(1, 17):
            if hk in ada:
                seq.append(("ada", l + 1, ada[hk]))
            if hk % 2 == 1:
                seq.append(("cat", l, 5 + (hk - 1) // 2))
        seq += [("cat", l, 3), ("cat", l, 4)]
        seq += [("cat", l, b) for b in range(13, NBLK)]
        return seq

    wseq = []
    for fb in range(12):
        wseq.append(("ada", 0, fb))
    for l in range(L):
        wseq.append(("cat", l, 0))
        last = layers[l] == DEPTH - 1
        for gi in GORDER:
            if groups[gi][2] and last:
                continue
            wseq += group_take_order(l, gi)
    wstate = {"issued": 0}
    cast_done = set()
    cast_hist = []

    def emit_cast(l, blk, after=()):
        if (l, blk) in cast_done:
            return
        cast_done.add((l, blk))
        cast_hist.append(buf("wbf_%d_%d" % (l, blk)))
        if len(cast_hist) > 4:
            after = list(after) + [cast_hist[-5]]
        src = d_wcat[l, blk].rearrange("(kc p) n -> p kc n", p=128)
        dst = d_wbf[l, blk].rearrange("p (kc n) -> p kc n", kc=KC)
        sch.dma("pool", "cast_%d_%d" % (l, blk), (lambda e, o=dst, i_=src: e.dma_start(out=o, in_=i_)),
                reads=list(after), writes=[buf("wbf_%d_%d" % (l, blk))])

    cast_order = [0, 1, 2, 5, 6, 7, 8, 9, 10, 11, 12, 3, 4] + list(range(13, 23))

    def w_issue_upto(k):
        while wstate["issued"] < min(k, len(wseq)):
            i = wstate["issued"]
            kind, l, idx = wseq[i]
            s = i % NSLOT
            if kind == "ada":
                src = d_wada[l, :, idx * 256:(idx + 1) * 256].rearrange("(kc p) n -> p kc n", p=128)
                dst = wsl_f[s][:, :, :]
                sch.dma("sp", "wa%d" % s, (lambda e, o=dst, i_=src: e.dma_start(out=o, in_=i_)), writes=[b_ws[s]])
            else:
                emit_cast(l, idx)
                src = d_wbf[l, idx].rearrange("p (kc n) -> p kc n", kc=KC)
                dst = wsl[s][:, :, :]
                sch.dma("sp", "ws%d" % s, (lambda e, o=dst, i_=src: e.dma_start(out=o, in_=i_)),
                        reads=[buf("wbf_%d_%d" % (l, idx))], writes=[b_ws[s]])
            wstate["issued"] += 1

    wcur = {"i": 0}

    def w_take(kind, l, idx):
        i = wcur["i"]
        assert wseq[i] == (kind, l, idx), (wseq[i], kind, l, idx)
        assert i < wstate["issued"] or i < NSLOT or wrel["k"] >= i - NSLOT, ("slot still busy", i, wrel["k"])
        w_issue_upto(i + 1)
        wcur["i"] += 1
        return i % NSLOT, i

    wdone = set()
    wrel = {"k": -1}

    def w_done(i):
        wdone.add(i)
        while (wrel["k"] + 1) in wdone:
            wrel["k"] += 1
        w_issue_upto(wrel["k"] + NSLOT + 1)

    for blk in cast_order[:3]:
        emit_cast(0, blk)
    w_issue_upto(NSLOT)

    def mod_ap(l, f, col):
        o = l * 48 + f * 2 + col
        return c_mod[:, o:o + 1]

    def gs_ap(l, kc, col):
        o = l * 16 + kc * 2 + col
        return c_gs[:, o:o + 1]

    def mm_group(out_ap, pairs, bank, reads):
        n = len(pairs)
        fns = []
        for k, (lt, rh) in enumerate(pairs):
            fns.append(lambda e, lt=lt, rh=rh, k=k: e.matmul(out_ap, lhsT=lt, rhs=rh, start=(k == 0), stop=(k == n - 1)))
        sch.op("pe", fns, reads=reads, writes=[b_ps[bank]])

    def mm_list(out_ap, pairs, bank, reads):
        n = len(pairs)
        lst = []
        for k, (lt, rh) in enumerate(pairs):
            lst.append(lambda k=k, lt=lt, rh=rh: sch.op(
                "pe", [lambda e: e.matmul(out_ap, lhsT=lt, rhs=rh, start=(k == 0), stop=(k == n - 1))],
                reads=reads, writes=[b_ps[bank]]))
        return lst

    sc3 = c_sc[:, :].rearrange("p (kc c) -> p kc c", c=2)

    def stage_mod_block(l, fb):
        bank = 7
        s_, wi = w_take("ada", l, fb)
        wv = wsl_f[s_]
        for j in range(2):
            mm_group(ps[bank][:, j * 2:j * 2 + 2],
                     [(wv[:, kc, j * 128:(j + 1) * 128], sc3[:, kc, :]) for kc in range(KC)],
                     bank, [b_ws[s_], b_sc])
        w_done(wi)
        f0 = fb * 2
        sch.op("dve", [lambda e: e.tensor_tensor(
            out=c_mod[:, l * 48 + f0 * 2:l * 48 + f0 * 2 + 4].rearrange("p (f c) -> p f c", c=2),
            in0=ps[bank][:, 0:4].rearrange("p (f c) -> p f c", c=2),
            in1=c_bada[:, l * 24 + f0:l * 24 + f0 + 2].unsqueeze(2).to_broadcast([128, 2, 2]),
            op=ALU.add)], reads=[b_ps[bank], b_const], writes=[b_mod[l]])

    def stage_mod_finish(l):
        sch.op("dve", [lambda e: e.scalar_tensor_tensor(
            out=c_gs[:, l * 16:(l + 1) * 16].rearrange("p (f c) -> p f c", c=2),
            in0=c_mod[:, l * 48 + 16:l * 48 + 32].rearrange("p (f c) -> p f c", c=2),
            scalar=1.0,
            in1=c_ng[:, l * 8:(l + 1) * 8].unsqueeze(2).to_broadcast([128, 8, 2]),
            op0=ALU.add, op1=ALU.mult)], reads=[b_mod[l], b_const], writes=[b_mod[l]])
        sm1, bsm1 = smalls()
        sch.op("dve", [lambda e: e.reduce_max(out=sm1[:, 0:1], in_=c_qg[:, l * 64:(l + 1) * 64], axis=AX.X,
                                              apply_absolute_value=True)], reads=[b_const], writes=[bsm1])
        sch.op("dve", [lambda e: e.reduce_max(out=sm1[:, 1:2], in_=c_kg[:, l * 64:(l + 1) * 64], axis=AX.X,
                                              apply_absolute_value=True)], reads=[b_const], writes=[bsm1])
        sch.op("dve", [lambda e: e.tensor_scalar(out=c_negB[:, l:l + 1], in0=sm1[:, 0:1], scalar1=sm1[:, 1:2],
                                                 scalar2=-8.0, op0=ALU.mult, op1=ALU.mult)],
               reads=[bsm1], writes=[b_negB[l]])


    STOP = stop
    def stage_norm(l, gi):
        t0, n, _ = groups[gi]
        bank = 6
        for kc in range(KC):
            sq, bsq = scratch()
            sch.op("act", [lambda e, sq=sq, kc=kc: e.activation(out=sq[:, 0:n], in_=xT[:, kc, t0:t0 + n], func=AF.Square)],
                   reads=[xb(gi, kc)], writes=[bsq])
            sch.op("pe", [lambda e, sq=sq, kc=kc: e.matmul(ps[bank][:, 0:n], lhsT=c_ones[:, :], rhs=sq[:, 0:n],
                                                           start=(kc == 0), stop=(kc == KC - 1))],
                   reads=[bsq, b_misc], writes=[b_ps[bank]])
        vt, bvt = scratch()
        sch.op("act", [lambda e: e.activation(out=vt[:, 0:n], in_=ps[bank][:, 0:n], func=AF.Ln, scale=1.0 / D, bias=c_eps[:, 0:1])],
               reads=[b_ps[bank], b_misc], writes=[bvt])
        sch.op("act", [lambda e: e.activation(out=rstd_all[:, t0:t0 + n], in_=vt[:, 0:n], func=AF.Exp, scale=-0.5)],
               reads=[bvt], writes=[b_rstd[gi]])

    def stage_hT(l, gi, hs, extra):
        t0, n, isctx = groups[gi]
        col = 1 if isctx else 0
        m = n + extra
        rd_extra = [b_rstd[gi + 1]] if extra else []
        for kc in range(KC):
            tmp, btmp = scratch()
            rds = [xb(gi, kc), b_rstd[gi]] + rd_extra + ([xb(gi + 1, kc)] if extra else [])
            sch.op("dve", [lambda e, tmp=tmp, kc=kc: e.tensor_tensor(out=tmp[:, 0:m], in0=xT[:, kc, t0:t0 + m],
                                                                    in1=rstd_all[:, t0:t0 + m], op=ALU.mult)],
                   reads=rds, writes=[btmp])
            sch.op("act", [lambda e, tmp=tmp, kc=kc: e.activation(out=hT[hs][:, kc, 0:m], in_=tmp[:, 0:m], func=AF.Identity,
                                                                  scale=gs_ap(l, kc, col), bias=mod_ap(l, kc, col))],
                   reads=[btmp, b_mod[l]], writes=[b_hT[hs][kc]])

    def rms_rope_a(src, bsrc, nh):
        w = nh * 64
        sq, bsq = scratch()
        sch.op("act", [lambda e: e.activation(out=sq[:, 0:w], in_=src[:, 0:w], func=AF.Square)],
               reads=[bsrc], writes=[bsq])
        return sq, bsq

    def rms_rope(l, src, bsrc, nh, gvec, tile_lat, perm, pre=None, gpvec=None):
        w = nh * 64
        sq, bsq = pre if pre is not None else rms_rope_a(src, bsrc, nh)
        ss, bss = smalls()
        sch.op("dve", [lambda e: e.reduce_sum(out=ss[:, 0:nh], in_=sq[:, 0:w].rearrange("p (h d) -> p h d", d=64), axis=AX.X)],
               reads=[bsq], writes=[bss])
        sch.op("act", [lambda e: e.activation(out=ss[:, 0:nh], in_=ss[:, 0:nh], func=AF.Ln, scale=1.0 / HD, bias=c_eps[:, 0:1])],
               reads=[bss, b_misc], writes=[bss])
        rr, brr = smalls()
        sch.op("act", [lambda e: e.activation(out=rr[:, 0:nh], in_=ss[:, 0:nh], func=AF.Exp, scale=-0.5)],
               reads=[bss], writes=[brr])
        src3 = src[:, 0:w].rearrange("p (h d) -> p h d", d=64)
        qn, bqn = scratch()
        qn3 = qn[:, 0:w].rearrange("p (h d) -> p h d", d=64)
        sch.op("dve", [lambda e: e.tensor_tensor(out=qn3, in0=src3, in1=rr[:, 0:nh].unsqueeze(2).to_broadcast([128, nh, 64]),
                                                 op=ALU.mult)], reads=[bsrc, brr], writes=[bqn])
        def out_views(dst_bf, a_in, b_in):
            if not perm:
                return [(dst_bf, a_in, b_in)]
            res = []
            for i in range(2):
                o = dst_bf[:, i * 512:(i + 1) * 512].rearrange("p (j r d) -> p r j d", j=4, r=2, d=64)
                a = a_in[:, i * 512:(i + 1) * 512].rearrange("p (r j d) -> p r j d", j=4, r=2, d=64)
                b = b_in[:, i * 512:(i + 1) * 512].rearrange("p (r j d) -> p r j d", j=4, r=2, d=64) if b_in is not None else None
                res.append((o, a, b))
            return res

        if tile_lat is None:
            dst, bdst = scratch()
            dst_bf = dst.bitcast(BF16)[:, 0:w]
            fns = []
            for o, a, _ in out_views(dst_bf, qn[:, 0:w], None):
                if perm:
                    gb = gvec.unsqueeze(1).unsqueeze(1).to_broadcast([128, 2, 4, 64])
                else:
                    o = o.rearrange("p (h d) -> p h d", d=64)
                    a = a.rearrange("p (h d) -> p h d", d=64)
                    gb = gvec.unsqueeze(1).to_broadcast([128, nh, 64])
                fns.append(lambda e, o=o, a=a, gb=gb: e.tensor_tensor(out=o, in0=a, in1=gb, op=ALU.mult))
            sch.op("dve", fns, reads=[bqn, b_const], writes=[bdst])
            return dst_bf, bdst
        ci = cgsg_ring.next()
        cg_t, bcg = cgsg[ci], b_cgsg[ci]
        sch.op("dve", [lambda e: e.tensor_tensor(out=cg_t[:, 0:64], in0=c_cos[:, tile_lat * 64:(tile_lat + 1) * 64], in1=gvec, op=ALU.mult),
                       lambda e: e.tensor_tensor(out=cg_t[:, 64:128], in0=c_sin[:, tile_lat * 64:(tile_lat + 1) * 64], in1=gpvec, op=ALU.mult)],
               reads=[b_const], writes=[bcg])
        cosv = cg_t[:, 0:64]
        sinv = cg_t[:, 64:128]
        t1, bt1 = scratch()
        t13 = t1[:, 0:w].rearrange("p (h d) -> p h d", d=64)
        sch.op("dve", [lambda e: e.tensor_tensor(out=t13, in0=qn3, in1=cosv.unsqueeze(1).to_broadcast([128, nh, 64]),
                                                 op=ALU.mult)], reads=[bqn, bcg], writes=[bt1])
        t2, bt2 = scratch()
        qn5 = qn[:, 0:w].rearrange("p (h a b d) -> p h a b d", a=2, b=2, d=16)
        t25 = t2[:, 0:w].rearrange("p (h a b d) -> p h a b d", a=2, b=2, d=16)
        sin4 = sinv.rearrange("p (a b d) -> p a b d", a=2, b=2, d=16)
        fns = []
        for hh in range(2):
            for bb in range(2):
                fns.append(lambda e, hh=hh, bb=bb: e.tensor_tensor(
                    out=t25[:, :, hh, bb, :], in0=qn5[:, :, hh, 1 - bb, :],
                    in1=sin4[:, hh, bb, :].unsqueeze(1).to_broadcast([128, nh, 16]), op=ALU.mult))
        sch.op("dve", fns, reads=[bqn, bcg], writes=[bt2])
        dst, bdst = scratch()
        dst_bf = dst.bitcast(BF16)[:, 0:w]
        fns = []
        for o, a, b in out_views(dst_bf, t1[:, 0:w], t2[:, 0:w]):
            fns.append(lambda e, o=o, a=a, b=b: e.tensor_tensor(out=o, in0=a, in1=b, op=ALU.add))
        sch.op("dve", fns, reads=[bt1, bt2], writes=[bdst])
        return dst_bf, bdst

    def pipeline(phases_per_tile):
        nt = len(phases_per_tile)
        phases_per_tile[0][0]()
        phases_per_tile[0][3]()
        phases_per_tile[0][1]()
        for t in range(nt):
            if t + 1 < nt:
                phases_per_tile[t + 1][0]()
                phases_per_tile[t + 1][3]()
            phases_per_tile[t][2]()
            if t + 1 < nt:
                phases_per_tile[t + 1][1]()

    def stage_kv(l, gi, hs, slot):
        t0, n, isctx = groups[gi]
        tiles = []
        for ti in range(n // 128):
            kt = t0 // 128 + ti
            st = {}

            def f1(ti=ti, kt=kt, st=st):
                bank = 4 + (kt % 2)
                mm_group(ps[bank][:, :],
                         [(hT[hs][:, kc, ti * 128:(ti + 1) * 128], wsl[slot][:, kc, :]) for kc in range(KC)],
                         bank, b_hT[hs] + [b_ws[slot]])
                kv, bkv = scratch()
                st["kv"], st["bkv"] = kv, bkv
                sch.op("act", [lambda e, kv=kv, bank=bank: e.activation(out=kv[:, 0:512], in_=ps[bank][:, :], func=AF.Copy)],
                       reads=[b_ps[bank]], writes=[bkv])

            def f2(ti=ti, kt=kt, st=st):
                kv, bkv = st["kv"], st["bkv"]
                sch.op("dve", [lambda e, kv=kv, kt=kt: e.tensor_copy(out=Vt5[:, kt, :, 0:3:2, :],
                                                                     in_=kv[:, 256:512].rearrange("p (m r d) -> p m r d", m=2, r=2, d=64))],
                       reads=[bkv], writes=[b_V[kt]])
                st["kr"], st["bkr"] = rms_rope(l, kv, bkv, NKV, c_kg[:, l * 64:(l + 1) * 64],
                                               None if isctx else (kt - CTX // 128), False, pre=st["pre"],
                                               gpvec=c_kgp[:, l * 64:(l + 1) * 64])

            def f2a(ti=ti, kt=kt, st=st):
                st["pre"] = rms_rope_a(st["kv"], st["bkv"], NKV)

            def f3(ti=ti, kt=kt, st=st):
                kr_bf, bkr = st["kr"], st["bkr"]
                tb = 7
                fns = []
                for j in range(2):
                    fns.append(lambda e, j=j, kr_bf=kr_bf: e.transpose(out=ps_bf[tb][:, j * 128:(j + 1) * 128],
                                                                       in_=kr_bf[:, j * 128:(j + 1) * 128], identity=c_ident[:, :]))
                sch.op("pe", fns, reads=[bkr, b_ident], writes=[b_ps[tb]])
                fns = []
                for r in range(2):
                    fns.append(lambda e, kt=kt, r=r: e.activation(
                        out=kT[r * 64:(r + 1) * 64, r:4:2, kt * 128:(kt + 1) * 128],
                        in_=ps_bf[tb][r * 64:(r + 1) * 64, 0:256].rearrange("p (j t) -> p j t", j=2), func=AF.Copy))
                sch.op("act", fns, reads=[b_ps[tb]], writes=[b_kT[kt]])

            tiles.append((f1, f2, f3, f2a))
        pipeline(tiles)

    def stage_q(l, gi, hs, s0, s1, qb_i):
        t0, n, isctx = groups[gi]
        qT = gbuf[qb_i]
        tiles = []
        for ti in range(n // 128):
            st = {}

            def f1(ti=ti, st=st):
                q_sb, bq = scratch()
                st["q"], st["bq"] = q_sb, bq
                for blk, sl in enumerate((s0, s1)):
                    bank = 4 + blk
                    mm_group(ps[bank][:, :],
                             [(hT[hs][:, kc, ti * 128:(ti + 1) * 128], wsl[sl][:, kc, :]) for kc in range(KC)],
                             bank, b_hT[hs] + [b_ws[sl]])
                    sch.op("act", [lambda e, blk=blk, bank=bank, q_sb=q_sb: e.activation(out=q_sb[:, blk * 512:(blk + 1) * 512],
                                                                                         in_=ps[bank][:, :], func=AF.Copy)],
                           reads=[b_ps[bank]], writes=[bq])

            def f2(ti=ti, st=st):
                tile_lat = None if isctx else (t0 - CTX) // 128 + ti
                st["qr"], st["bqr"] = rms_rope(l, st["q"], st["bq"], NH, c_qg[:, l * 64:(l + 1) * 64], tile_lat, True,
                                               pre=st["pre"], gpvec=c_qgp[:, l * 64:(l + 1) * 64])

            def f2a(ti=ti, st=st):
                st["pre"] = rms_rope_a(st["q"], st["bq"], NH)

            def f3(ti=ti, st=st):
                qr_bf, bqr = st["qr"], st["bqr"]
                tb = 7
                fns = []
                for c in range(8):
                    fns.append(lambda e, c=c, qr_bf=qr_bf: e.transpose(out=ps_bf[tb][:, c * 128:(c + 1) * 128],
                                                                       in_=qr_bf[:, c * 128:(c + 1) * 128], identity=c_ident[:, :]))
                sch.op("pe", fns, reads=[bqr, b_ident], writes=[b_ps[tb]])
                sch.op("act", [lambda e, ti=ti: e.activation(out=qT[:, :, ti * 128:(ti + 1) * 128],
                                                             in_=ps_bf[tb][:, :].rearrange("p (c t) -> p c t", c=8),
                                                             func=AF.Copy)], reads=[b_ps[tb]], writes=b_g[qb_i])

            tiles.append((f1, f2, f3, f2a))
        pipeline(tiles)

    def stage_att(l, gi, q_i, a_i, feeder=None):
        t0, n, isctx = groups[gi]
        qT = gbuf[q_i]
        ga = gbuf[a_i]
        kts = [0, 1] if isctx else list(range(18))
        pairs = [(kts[2 * p], kts[2 * p + 1]) for p in range(len(kts) // 2)]
        npair = len(pairs)
        iters = [(g, qb) for g in range(NKV) for qb in range(n // 128)]
        steps = [(it, p) for it in range(len(iters)) for p in range(npair)]

        def emit_S(k):
            it, p = steps[k]
            g, qb = iters[it]
            i = g // 2
            sp = k % 2
            fns = []
            for h, kt in enumerate(pairs[p]):
                fns.append(lambda e, kt=kt, h=h, sp=sp, g=g, i=i, qb=qb: e.matmul(
                    psS[sp][:, h * 512:(h + 1) * 512], lhsT=kT[:, g, kt * 128:(kt + 1) * 128],
                    rhs=qT[:, 4 * i:4 * i + 4, qb * 128:(qb + 1) * 128], start=True, stop=True))
            sch.op("pe", fns, reads=[b_kT[pairs[p][0]], b_kT[pairs[p][1]]] + b_g[q_i], writes=[b_ps[2 * sp], b_ps[2 * sp + 1]])

        emit_S(0)
        if len(steps) > 1:
            emit_S(1)
        for k, (it, p) in enumerate(steps):
            g, qb = iters[it]
            if feeder and p == 0:
                feeder.start(it + 1)
            r = g % 2
            voff = (g // 2) * 192 + r * 64
            nump = r * 64
            denp = (1 - r) * 64
            ob = 4 + (it % 2)
            sp = k % 2
            pt, bpt = scratch()
            pt_bf = pt.bitcast(BF16)
            sch.op("act", [lambda e, pt_bf=pt_bf, sp=sp: e.activation(
                out=pt_bf[:, 0:1024], in_=psS[sp][:, 0:1024], func=AF.Exp, scale=HD ** -0.5, bias=c_negB[:, l:l + 1])],
                reads=[b_ps[2 * sp], b_ps[2 * sp + 1], b_negB[l]], writes=[bpt])
            if k + 2 < len(steps):
                emit_S(k + 2)
            fns = []
            for h, kt in enumerate(pairs[p]):
                first = (p == 0 and h == 0)
                lastm = (p == npair - 1 and h == 1)
                fns.append(lambda e, kt=kt, h=h, pt_bf=pt_bf, ob=ob, voff=voff, first=first, lastm=lastm: e.matmul(
                    ps[ob][:, :], lhsT=Vt[:, kt, voff:voff + 128], rhs=pt_bf[:, h * 512:(h + 1) * 512],
                    start=first, stop=lastm))
            sch.op("pe", fns, reads=[bpt, b_V[pairs[p][0]], b_V[pairs[p][1]]], writes=[b_ps[ob]])
            if feeder:
                feeder.pump(2)
            if p == npair - 1:
                rd, brd = rd_t, b_rd
                sch.op("dve", [lambda e, ob=ob, rd=rd, nump=nump, denp=denp: e.reciprocal(
                    out=rd[nump:nump + 64, 0:512], in_=ps[ob][denp:denp + 64, :])], reads=[b_ps[ob]], writes=[brd])
                o4 = ps[ob][nump:nump + 64, :].rearrange("p (j q) -> p j q", j=4)
                r4 = rd[nump:nump + 64, 0:512].rearrange("p (j q) -> p j q", j=4)
                fns = []
                for par in range(2):
                    fns.append(lambda e, par=par, o4=o4, r4=r4, g=g, qb=qb: e.tensor_tensor(
                        out=ga[par * 64:(par + 1) * 64, 2 * g:2 * g + 2, qb * 128:(qb + 1) * 128],
                        in0=o4[:, par:4:2, :], in1=r4[:, par:4:2, :], op=ALU.mult))
                sch.op("dve", fns, reads=[b_ps[ob], brd], writes=[b_g[a_i][2 * g], b_g[a_i][2 * g + 1]])
                if feeder:
                    feeder.finish()

    def stage_za(l, gi, hs, s0, s1, a_i):
        t0, n, isctx = groups[gi]
        ga = gbuf[a_i]
        for c in range(8):
            sl = (s0, s1)[c // 4]
            cols = (c % 4) * 128
            bank = 4 + (c % 4)
            mm_group(ps[bank][:, 0:n], [(wsl[sl][:, kc, cols:cols + 128], hT[hs][:, kc, 0:n]) for kc in range(KC)],
                     bank, b_hT[hs] + [b_ws[sl]])
            sz, bsz = scratch()
            sch.op("act", [lambda e, sz=sz, bank=bank: e.activation(out=sz[:, 0:n], in_=ps[bank][:, 0:n], func=AF.Silu)],
                   reads=[b_ps[bank]], writes=[bsz])
            sch.op("dve", [lambda e, sz=sz, c=c: e.tensor_tensor(out=ga[:, c, 0:n], in0=ga[:, c, 0:n], in1=sz[:, 0:n],
                                                                 op=ALU.mult)],
                   reads=[bsz, b_g[a_i][c]], writes=[b_g[a_i][c]])

    def conv_half_a(l, gi, hs, c, slot, bk=(6, 7)):
        t0, n, isctx = groups[gi]
        ot = 1 - hs
        u = hT_f[ot][:, 0:514]
        y = hT_f[ot][:, 514:1026]
        bu = b_hT[ot]
        first = (gi == 1)
        lastg = (gi == NG - 1)
        if isctx:
            off, m, a = 0, n, 1
        else:
            off, m, a = 1, (n - 1 if lastg else n), 2
        mms = []
        for k, bank in ((0, bk[0]), (1, bk[1])):
            mms += mm_list(ps[bank][:, 0:m], [(wsl[slot][:, kc, k * 128:(k + 1) * 128], hT[hs][:, kc, off:off + m]) for kc in range(KC)],
                           bank, b_hT[hs] + [b_ws[slot]])
        return mms, (lambda: conv_evac_a(l, gi, hs, c, slot, bk))

    def conv_evac_a(l, gi, hs, c, slot, bk=(6, 7)):
        t0, n, isctx = groups[gi]
        ot = 1 - hs
        u = hT_f[ot][:, 0:514]
        y = hT_f[ot][:, 514:1026]
        bu = b_hT[ot]
        first = (gi == 1)
        lastg = (gi == NG - 1)
        if isctx:
            off, m, a = 0, n, 1
        else:
            off, m, a = 1, (n - 1 if lastg else n), 2
        xs, bxs = scratch()
        sch.op("act", [lambda e: e.activation(out=xs[:, 0:m], in_=ps[bk[0]][:, 0:m], func=AF.Copy)], reads=[b_ps[bk[0]]], writes=[bxs])
        sch.op("dve", [lambda e: e.tensor_tensor(out=u[:, a:a + m], in0=ps[bk[1]][:, 0:m], in1=xs[:, 0:m], op=ALU.mult)],
               reads=[b_ps[bk[1]], bxs], writes=bu)
        if isctx:
            sch.op("dve", [lambda e: e.memset(u[:, 0:1], 0.0), lambda e: e.memset(u[:, n + 1:n + 2], 0.0)], writes=bu)
        else:
            if first:
                for k, bank in ((0, bk[0]), (1, bk[1])):
                    mm_group(ps[bank][:, 0:1], [(wsl[slot][:, kc, k * 128:(k + 1) * 128], hT[hs][:, kc, 0:1]) for kc in range(KC)],
                             bank, b_hT[hs] + [b_ws[slot]])
                hx, bhx = smalls()
                sch.op("act", [lambda e: e.activation(out=hx[:, 0:1], in_=ps[bk[0]][:, 0:1], func=AF.Copy)], reads=[b_ps[bk[0]]], writes=[bhx])
                sch.op("dve", [lambda e: e.tensor_tensor(out=u[:, 1:2], in0=ps[bk[1]][:, 0:1], in1=hx[:, 0:1], op=ALU.mult)],
                       reads=[b_ps[bk[1]], bhx], writes=bu)
                sch.op("dve", [lambda e: e.memset(u[:, 0:1], 0.0)], writes=bu)
            else:
                sch.op("dve", [lambda e: e.tensor_copy(out=u[:, 0:2], in_=uprev[:, 2 * c:2 * c + 2])], reads=[b_uprev[c]], writes=bu)
            if lastg:
                sch.op("dve", [lambda e: e.memset(u[:, n + 1:n + 2], 0.0)], writes=bu)
            else:
                sch.op("dve", [lambda e: e.tensor_copy(out=uprev[:, 2 * c:2 * c + 2], in_=u[:, n:n + 2])], reads=bu, writes=[b_uprev[c]])
        cw = lambda k: c_convw[:, l * 24 + k * 8 + c:l * 24 + k * 8 + c + 1]
        sch.op("dve", [lambda e: e.tensor_scalar(out=y[:, 0:n], in0=u[:, 0:n], scalar1=cw(0), scalar2=None, op0=ALU.mult)],
               reads=bu + [b_const], writes=bu)
        sch.op("dve", [lambda e: e.scalar_tensor_tensor(out=y[:, 0:n], in0=u[:, 1:n + 1], scalar=cw(1), in1=y[:, 0:n],
                                                        op0=ALU.mult, op1=ALU.add)], reads=bu + [b_const], writes=bu)
        sch.op("dve", [lambda e: e.scalar_tensor_tensor(out=y[:, 0:n], in0=u[:, 2:n + 2], scalar=cw(2), in1=y[:, 0:n],
                                                        op0=ALU.mult, op1=ALU.add)], reads=bu + [b_const], writes=bu)

    def conv_half_b(l, gi, hs, c, slot, y_i, bk=(6, 7)):
        t0, n, isctx = groups[gi]
        ot = 1 - hs
        y = hT_f[ot][:, 514:1026]
        bu = b_hT[ot]
        yz = gbuf[y_i]
        mms = []
        for k, bank in ((2, bk[0]), (3, bk[1])):
            mms += mm_list(ps[bank][:, 0:n], [(wsl[slot][:, kc, k * 128:(k + 1) * 128], hT[hs][:, kc, 0:n]) for kc in range(KC)],
                           bank, b_hT[hs] + [b_ws[slot]])
        return mms, (lambda: conv_evac_b(l, gi, hs, c, slot, y_i, bk))

    def conv_evac_b(l, gi, hs, c, slot, y_i, bk=(6, 7)):
        t0, n, isctx = groups[gi]
        ot = 1 - hs
        y = hT_f[ot][:, 514:1026]
        bu = b_hT[ot]
        yz = gbuf[y_i]
        cb = c_convb[:, l * 8 + c:l * 8 + c + 1]
        sch.op("dve", [lambda e: e.scalar_tensor_tensor(out=y[:, 0:n], in0=y[:, 0:n], scalar=cb, in1=ps[bk[0]][:, 0:n],
                                                        op0=ALU.add, op1=ALU.mult)], reads=bu + [b_ps[bk[0]], b_const], writes=bu)
        tz, btz = scratch()
        sch.op("act", [lambda e: e.activation(out=tz[:, 0:n], in_=ps[bk[1]][:, 0:n], func=AF.Tanh, scale=0.5)],
               reads=[b_ps[bk[1]]], writes=[btz])
        sch.op("dve", [lambda e: e.scalar_tensor_tensor(out=tz[:, 0:n], in0=tz[:, 0:n], scalar=1.0, in1=ps[bk[1]][:, 0:n],
                                                        op0=ALU.add, op1=ALU.mult)], reads=[btz, b_ps[bk[1]]], writes=[btz])
        sch.op("dve", [lambda e: e.scalar_tensor_tensor(out=yz[:, c, 0:n], in0=y[:, 0:n], scalar=0.5, in1=tz[:, 0:n],
                                                        op0=ALU.mult, op1=ALU.mult)], reads=bu + [btz], writes=[b_g[y_i][c]])

    def stage_merge(l, gi, hs, j, slot, a_i, y_i, m_i):
        t0, n, isctx = groups[gi]
        pb = (j % 2) * 4
        srcs = [(gbuf[a_i], b_g[a_i]), (gbuf[y_i], b_g[y_i]), (hT[hs], b_hT[hs]), (hT[hs], b_hT[hs])]
        for k in range(4):
            bank = pb + k
            src, bsrc = srcs[k]
            mm_group(ps[bank][:, 0:n], [(wsl[slot][:, kc, k * 128:(k + 1) * 128], src[:, kc, 0:n]) for kc in range(KC)],
                     bank, bsrc + [b_ws[slot]])
        s0, bs0 = scratch()
        s1, bs1 = scratch()
        for k, (sx, bsx) in enumerate(((s0, bs0), (s1, bs1))):
            bg = c_bgate[:, l * 16 + k * 8 + j:l * 16 + k * 8 + j + 1]
            sch.op("act", [lambda e, sx=sx, k=k, bg=bg: e.activation(out=sx[:, 0:n], in_=ps[pb + 2 + k][:, 0:n],
                                                                     func=AF.Sigmoid, bias=bg)],
                   reads=[b_ps[pb + 2 + k], b_const], writes=[bsx])
        sch.op("dve", [lambda e: e.tensor_tensor(out=s0[:, 0:n], in0=s0[:, 0:n], in1=ps[pb][:, 0:n], op=ALU.mult)],
               reads=[bs0, b_ps[pb]], writes=[bs0])
        sch.op("dve", [lambda e: e.tensor_tensor(out=s1[:, 0:n], in0=s1[:, 0:n], in1=ps[pb + 1][:, 0:n], op=ALU.mult)],
               reads=[bs1, b_ps[pb + 1]], writes=[bs1])
        sch.op("dve", [lambda e: e.tensor_tensor(out=gbuf[m_i][:, j, 0:n], in0=s0[:, 0:n], in1=s1[:, 0:n], op=ALU.add)],
               reads=[bs0, bs1], writes=[b_g[m_i][j]])

    def stage_out(l, gi, s0, s1, m_i):
        t0, n, isctx = groups[gi]
        col = 1 if isctx else 0
        mg = gbuf[m_i]
        for j in range(8):
            sl = (s0, s1)[j // 4]
            cols = (j % 4) * 128
            bank = j % 4
            mm_group(ps[bank][:, 0:n], [(wsl[sl][:, kc, cols:cols + 128], mg[:, kc, 0:n]) for kc in range(KC)],
                     bank, b_g[m_i] + [b_ws[sl]])
            sch.op("dve", [lambda e, j=j, bank=bank: e.scalar_tensor_tensor(
                out=xT[:, j, t0:t0 + n], in0=ps[bank][:, 0:n], scalar=mod_ap(l, 16 + j, col), in1=xT[:, j, t0:t0 + n],
                op0=ALU.mult, op1=ALU.add)], reads=[b_ps[bank], xb(gi, j), b_mod[l]], writes=[xb(gi, j)])

    outT_dr = d_out.rearrange("(kc p) t -> p kc t", p=128)

    def store_group(gi):
        t0, n, _ = groups[gi]
        sch.dma("sp", "out", (lambda e: e.dma_start(out=outT_dr[:, :, t0:t0 + n], in_=xT[:, :, t0:t0 + n])),
                reads=xbs(gi), writes=[b_out])

    norm_done = set()
    for gi in range(NG):
        stage_norm(0, gi)
        norm_done.add((0, gi))
    for fb in range(12):
        stage_mod_block(0, fb)
    stage_mod_finish(0)

    hring = Ring([0, 1])
    pre_hT = {}
    for l in range(L):
        last = layers[l] == DEPTH - 1
        final = (l == L - 1)
        skv, wkv = w_take("cat", l, 0)
        sch.mark("L%d prepass" % l)
        for gi in range(NG):
            hs = hring.next()
            if (l, gi) not in norm_done:
                stage_norm(l, gi)
            stage_hT(l, gi, hs, 0)
            stage_kv(l, gi, hs, skv)
        w_done(wkv)
        if l == 0:
            for blk in cast_order[3:]:
                emit_cast(0, blk, after=[b_kT[17]])
        if STOP == "pre":
            break
        for gi in GORDER:
            t0, n, isctx = groups[gi]
            if isctx and last:
                if final:
                    store_group(gi)
                continue
            has_right = (not isctx) and gi < NG - 1
            has_left = (not isctx) and gi > 1
            sch.mark("L%d g%d hT+q" % (l, gi))
            if pre_hT.get((l, gi)) is None:
                hs = hring.next()
                stage_hT(l, gi, hs, 1 if has_right else 0)
            else:
                hs = pre_hT[(l, gi)]
            q_i, a_i, y_i = 0, 1, 2
            m_i = 0
            sq0, wq0 = w_take("cat", l, 1)
            sq1, wq1 = w_take("cat", l, 2)
            stage_q(l, gi, hs, sq0, sq1, q_i)
            w_done(wq0)
            w_done(wq1)
            if STOP == "q":
                break
            if l + 1 < L and gi != 1:
                per = (NBLK + (NG - 1) - 1) // (NG - 1)
                pos_c = GORDER.index(gi) - 1
                for blk in cast_order[pos_c * per:(pos_c + 1) * per]:
                    emit_cast(l + 1, blk)
            sch.mark("L%d g%d att" % (l, gi))
            ada = {}
            if l + 1 < L and isctx:
                ada = {hk: hk - 1 for hk in range(1, 13)}

            class Feeder:
                def __init__(self):
                    self.mms = []
                    self.evac = None
                    self.slot = None
                    self.wi = None

                def start(self, hk, l=l, gi=gi, hs=hs, y_i=y_i, ada=ada, isctx=isctx):
                    if hk in ada:
                        stage_mod_block(l + 1, ada[hk])
                    c = (hk - 1) // 2
                    bk = ((6, 5) if hk % 2 == 1 else (4, 3)) if isctx else (6, 7)
                    if hk % 2 == 1:
                        self.slot, self.wi = w_take("cat", l, 5 + c)
                        self.mms, self.evac = conv_half_a(l, gi, hs, c, self.slot, bk)
                        self.release = None
                    else:
                        self.mms, self.evac = conv_half_b(l, gi, hs, c, self.slot, y_i, bk)
                        self.release = self.wi

                def pump(self, k):
                    for _ in range(k):
                        if self.mms:
                            self.mms.pop(0)()
                    if not self.mms and self.evac:
                        ev, self.evac = self.evac, None
                        if self.release is not None:
                            w_done(self.release)
                        ev()

                def finish(self):
                    self.pump(10 ** 6)

            fd = Feeder()
            if isctx:
                stage_att(l, gi, q_i, a_i, None)
                for hk in range(1, 17):
                    fd.start(hk)
                    fd.finish()
            else:
                stage_att(l, gi, q_i, a_i, fd)
            if l + 1 < L and isctx:
                stage_mod_finish(l + 1)
            sch.mark("L%d g%d za" % (l, gi))
            sz0, wz0 = w_take("cat", l, 3)
            sz1, wz1 = w_take("cat", l, 4)
            stage_za(l, gi, hs, sz0, sz1, a_i)
            w_done(wz0)
            w_done(wz1)
            sch.mark("L%d g%d merge" % (l, gi))
            for j in range(8):
                sm_, wm_ = w_take("cat", l, 13 + j)
                stage_merge(l, gi, hs, j, sm_, a_i, y_i, m_i)
                w_done(wm_)
            order_l = [g_ for g_ in GORDER if not (groups[g_][2] and last)]
            pos = order_l.index(gi)
            if pos + 1 < len(order_l):
                gn = order_l[pos + 1]
                hs_n = hring.next()
                assert hs_n == 1 - hs
                stage_hT(l, gn, hs_n, 1 if ((not groups[gn][2]) and gn < NG - 1) else 0)
                pre_hT[(l, gn)] = hs_n
            sch.mark("L%d g%d out" % (l, gi))
            so0, wo0 = w_take("cat", l, 21)
            so1, wo1 = w_take("cat", l, 22)
            stage_out(l, gi, so0, so1, m_i)
            w_done(wo0)
            w_done(wo1)
            if l + 1 < L:
                stage_norm(l + 1, gi)
                norm_done.add((l + 1, gi))
            if final:
                store_group(gi)
    if STOP is None:
        assert wcur["i"] == len(wseq), (wcur["i"], len(wseq))
    else:
        for gi in range(NG):
            store_group(gi)
    sch.wait_all("sp", [b_out.w])

    sch.mark("end")
    with nc.Block() as block:
        sch.replay(block)
    nc._stage_marks = sch.marks
    return nc


def _rope_tables():
    t = np.arange(S)
    row = (t // 64).astype(np.float32)
    col = (t % 64).astype(np.float32)
    half = HD // 2
    inv = (np.float32(10000.0) ** (-np.arange(0, half, 2, dtype=np.float32) / np.float32(half))).astype(np.float32)
    ang_r = row[:, None] * inv[None, :]
    ang_c = col[:, None] * inv[None, :]
    ang = np.concatenate([ang_r, ang_r, ang_c, ang_c], axis=-1).astype(np.float32)
    cos = np.cos(ang).astype(np.float32)
    sin = np.sin(ang).astype(np.float32)
    sign = np.concatenate([-np.ones(16), np.ones(16), -np.ones(16), np.ones(16)]).astype(np.float32)
    sinS = sin * sign[None, :]

    def lay(a):
        return np.ascontiguousarray(a.reshape(16, 128, 64).transpose(1, 0, 2).reshape(128, 16 * 64))

    return lay(cos), lay(sinS)


def _pvec(v):
    v = np.asarray(v, dtype=np.float32)
    lead = v.shape[:-1]
    n = v.shape[-1] // 128
    a = v.reshape(lead + (n, 128))
    a = np.moveaxis(a, -1, 0)
    return np.ascontiguousarray(a.reshape(128, -1))


def _wcat(w_in, w_attn_br, w_conv_br, w_out):
    blocks = []
    q = w_in[:, 0:1024]
    k = w_in[:, 1024:1280]
    v = w_in[:, 1280:1536]
    za = w_in[:, 1536:2560]
    xc = w_in[:, 2560:3584]
    bc = w_in[:, 3584:4608]
    cc = w_in[:, 4608:5632]
    zc = w_in[:, 5632:6656]
    gl0 = w_in[:, 6656:7680]
    gl1 = w_in[:, 7680:8704]
    blocks.append(np.concatenate([k, v], axis=1))
    blocks += [q[:, 0:512], q[:, 512:1024], za[:, 0:512], za[:, 512:1024]]
    for c in range(8):
        s = slice(c * 128, (c + 1) * 128)
        blocks.append(np.concatenate([xc[:, s], cc[:, s], bc[:, s], zc[:, s]], axis=1))
    for j in range(8):
        s = slice(j * 128, (j + 1) * 128)
        blocks.append(np.concatenate([w_attn_br[:, s], w_conv_br[:, s], gl0[:, s], gl1[:, s]], axis=1))
    blocks += [w_out[:, 0:512], w_out[:, 512:1024]]
    return np.ascontiguousarray(np.stack(blocks, axis=0), dtype=np.float32)


_SWAP = np.concatenate([np.arange(16, 32), np.arange(0, 16), np.arange(48, 64), np.arange(32, 48)])

_PROG_CACHE = {}


def _get_prog(layers):
    key = tuple(layers)
    if key not in _PROG_CACHE:
        _PROG_CACHE[key] = build_program(list(layers))
    return _PROG_CACHE[key]


def _run_layers(layers, xall_T, inputs):
    f = lambda a: np.asarray(a, dtype=np.float32)
    L = len(layers)
    li = list(layers)
    shared = {
        "wada": np.ascontiguousarray(f(inputs["w_ada"])[li]),
        "bada": np.concatenate([_pvec(f(inputs["b_ada"])[l]) for l in li], axis=1),
        "ng": np.concatenate([_pvec(f(inputs["norm_g"])[l]) for l in li], axis=1),
        "wcat": np.stack([_wcat(f(inputs["w_in"])[l], f(inputs["w_attn_br"])[l], f(inputs["w_conv_br"])[l],
                                f(inputs["w_out"])[l]) for l in li], axis=0),
        "qg": np.concatenate([np.tile(f(inputs["q_norm_g"])[l][None, :], (128, 1)) for l in li], axis=1),
        "kg": np.concatenate([np.tile(f(inputs["k_norm_g"])[l][None, :], (128, 1)) for l in li], axis=1),
        "qgp": np.concatenate([np.tile(f(inputs["q_norm_g"])[l][_SWAP][None, :], (128, 1)) for l in li], axis=1),
        "kgp": np.concatenate([np.tile(f(inputs["k_norm_g"])[l][_SWAP][None, :], (128, 1)) for l in li], axis=1),
        "convw": np.concatenate([_pvec(f(inputs["conv_w"])[l]) for l in li], axis=1),
        "convb": np.concatenate([_pvec(f(inputs["conv_b"])[l]) for l in li], axis=1),
        "bgate": np.concatenate([_pvec(f(inputs["b_gate"])[l]) for l in li], axis=1),
        "ident": np.eye(128, dtype=np.float32),
    }
    cosT, sinT = _rope_tables()
    shared["cosT"] = cosT
    shared["sinT"] = sinT
    shared = {k: np.ascontiguousarray(v, dtype=np.float32) for k, v in shared.items()}
    c = f(inputs["c"])
    c_ctx = f(inputs["c_ctx"])
    in_maps = []
    for b in range(N_CORES):
        cv = np.stack([c[b].reshape(KC, 128).T, c_ctx.reshape(KC, 128).T], axis=-1)
        m = dict(shared)
        m["xT"] = np.ascontiguousarray(xall_T[b], dtype=np.float32)
        m["cvec"] = np.ascontiguousarray(cv.reshape(128, KC * 2), dtype=np.float32)
        in_maps.append(m)
    nc = _get_prog(layers)
    res = run_bass_kernel_spmd(nc, in_maps, core_ids=list(range(N_CORES)))
    return [np.asarray(r["outT"]) for r in res.results]


FUSED = True


def kernel(x, c, ctx, c_ctx, norm_g, w_ada, b_ada, w_in, q_norm_g, k_norm_g, conv_w, conv_b,
           w_attn_br, w_conv_br, b_gate, w_out):
    inputs = dict(x=x, c=c, ctx=ctx, c_ctx=c_ctx, norm_g=norm_g, w_ada=w_ada, b_ada=b_ada, w_in=w_in,
                  q_norm_g=q_norm_g, k_norm_g=k_norm_g, conv_w=conv_w, conv_b=conv_b, w_attn_br=w_attn_br,
                  w_conv_br=w_conv_br, b_gate=b_gate, w_out=w_out)
    x = np.asarray(x, dtype=np.float32)
    ctx = np.asarray(ctx, dtype=np.float32)
    xall_T = [np.ascontiguousarray(np.concatenate([ctx[b], x[b]], axis=0).T) for b in range(N_CORES)]
    if FUSED:
        outs = _run_layers([0, 1], xall_T, inputs)
    else:
        mid = _run_layers([0], xall_T, inputs)
        outs = _run_layers([1], mid, inputs)
    out = np.stack([o[:, CTX:].T for o in outs], axis=0)
    return np.ascontiguousarray(out, dtype=np.float32)
```

```python
import numpy as np
import concourse.bass as bass
import concourse.mybir as mybir
from concourse.bass_utils import run_bass_kernel_spmd

F32 = mybir.dt.float32
BF16 = mybir.dt.bfloat16
ALU = mybir.AluOpType
AF = mybir.ActivationFunctionType
AX = mybir.AxisListType

D = 1024
S = 2048
CTX = 256
T = S + CTX
DEPTH = 2
HD = 64
NH = 16
NKV = 4
KC = 8
NBLK = 23
EPS = 1e-6
G = 512
NSLOT = 3
NSCR = 4
N_CORES = 8


class Buf:
    __slots__ = ("name", "w", "r")

    def __init__(self, name):
        self.name = name
        self.w = None
        self.r = {}


class Sched:
    ENGS = ("pe", "act", "dve", "pool", "sp")

    def __init__(self, nc):
        self.nc = nc
        self.streams = {e: [] for e in self.ENGS}
        self.esem = {e: nc.alloc_semaphore("es_" + e) for e in ("pe", "act", "dve", "pool")}
        self.ecnt = {e: 0 for e in self.esem}
        self.waited = {e: {} for e in self.ENGS}
        self.dsem = {}
        self.dcnt = {}
        self.nops = 0
        self.npe = 0
        self.marks = []

    def _deps(self, e, reads, writes):
        need = {}

        def add(tok):
            if tok is None:
                return
            s, v = tok
            if need.get(s, 0) < v:
                need[s] = v

        for b in reads:
            add(b.w)
        for b in writes:
            add(b.w)
            for s, v in b.r.items():
                add((s, v))
        own = self.esem.get(e)
        out = []
        wd = self.waited[e]
        for s, v in need.items():
            if e == "pe" and s is own:
                continue
            if wd.get(s, 0) < v:
                wd[s] = v
                out.append((s, v))
        return out

    def _mark(self, tok, reads, writes):
        s, v = tok
        for b in reads:
            if b.r.get(s, 0) < v:
                b.r[s] = v
        for b in writes:
            b.w = tok
            b.r = {}

    def mark(self, name):
        self.marks.append((name, self.npe))

    def op(self, e, fns, reads=(), writes=()):
        if e == "pe":
            self.npe += len(fns)
        waits = self._deps(e, reads, writes)
        sem = self.esem[e]
        self.ecnt[e] += 1
        tok = (sem, self.ecnt[e])

        def emit(eng, waits=waits, fns=fns, sem=sem):
            for s, v in waits:
                eng.wait_ge(s, v)
            for f in fns[:-1]:
                f(eng)
            fns[-1](eng).then_inc(sem, 1)

        self.streams[e].append(emit)
        self._mark(tok, reads, writes)
        self.nops += 1
        return tok

    def op_noinc(self, e, fn, reads=(), writes=()):
        waits = self._deps(e, reads, writes)
        tok = (self.esem[e], self.ecnt[e] + 1)
        if e == "pe":
            self.npe += 1

        def emit(eng, waits=waits, fn=fn):
            for s, v in waits:
                eng.wait_ge(s, v)
            fn(eng)

        self.streams[e].append(emit)
        self._mark(tok, reads, writes)
        self.nops += 1
        return tok

    def dma(self, q, key, fn, reads=(), writes=()):
        if key not in self.dsem:
            self.dsem[key] = self.nc.alloc_semaphore("ds_" + key)
            self.dcnt[key] = 0
        waits = self._deps(q, reads, writes)
        sem = self.dsem[key]
        self.dcnt[key] += 16
        tok = (sem, self.dcnt[key])

        def emit(eng, waits=waits, fn=fn, sem=sem):
            for s, v in waits:
                eng.wait_ge(s, v)
            fn(eng).then_inc(sem, 16)

        self.streams[q].append(emit)
        self._mark(tok, reads, writes)
        return tok

    def wait_all(self, q, toks):
        def emit(eng, toks=toks):
            for s, v in toks:
                eng.wait_ge(s, v)

        self.streams[q].append(emit)

    def replay(self, block):
        st = self.streams

        @block.sync
        def _(e):
            for f in st["sp"]:
                f(e)

        @block.gpsimd
        def _(e):
            for f in st["pool"]:
                f(e)

        @block.tensor
        def _(e):
            for f in st["pe"]:
                f(e)

        @block.vector
        def _(e):
            for f in st["dve"]:
                f(e)

        @block.scalar
        def _(e):
            for f in st["act"]:
                f(e)


class Ring:
    def __init__(self, items):
        self.items = items
        self.i = 0

    def next(self):
        it = self.items[self.i % len(self.items)]
        self.i += 1
        return it


def build_program(layers, stop=None):
    L = len(layers)
    nc = bass.Bass("TRN2", target_bir_lowering=False)
    sch = Sched(nc)

    def dram(name, shape, dt=F32, kind="ExternalInput"):
        return nc.dram_tensor(name, list(shape), dt, kind=kind).ap()

    d_xT = dram("xT", [D, T])
    d_cvec = dram("cvec", [128, KC * 2])
    d_wada = dram("wada", [L, D, 3 * D])
    d_bada = dram("bada", [128, L * 24])
    d_ng = dram("ng", [128, L * 8])
    d_wcat = dram("wcat", [L, NBLK, D, 512])
    d_qg = dram("qg", [128, L * 64])
    d_kg = dram("kg", [128, L * 64])
    d_qgp = dram("qgp", [128, L * 64])
    d_kgp = dram("kgp", [128, L * 64])
    d_convw = dram("convw", [128, L * 3 * 8])
    d_convb = dram("convb", [128, L * 8])
    d_bgate = dram("bgate", [128, L * 2 * 8])
    d_cos = dram("cosT", [128, 16 * 64])
    d_sin = dram("sinT", [128, 16 * 64])
    d_ident = dram("ident", [128, 128])
    d_out = dram("outT", [D, T], kind="ExternalOutput")
    d_wbf = nc.dram_tensor("wbf", [L, NBLK, 128, KC * 512], BF16, kind="Internal").ap()

    def sb(name, shape, dt=F32):
        return nc.alloc_sbuf_tensor(name, list(shape), dt)

    xT = sb("xT_sb", [128, KC, T])
    rstd_all = sb("rstd_all", [128, T])
    kT = sb("kT_sb", [128, 4, T], BF16)
    Vt = sb("Vt_sb", [128, 18, 384], BF16)
    hT_raw = [sb("hT%d" % i, [128, KC * (G + 1) + 0], BF16) for i in range(2)]
    hT = [h[:, :].rearrange("p (kc t) -> p kc t", kc=KC) for h in hT_raw]
    hT_f = [h.bitcast(F32) for h in hT_raw]
    gbuf = [sb("gbuf%d" % i, [128, KC, G], BF16) for i in range(3)]
    wsl = [sb("wslot%d" % i, [128, KC, 512], BF16) for i in range(NSLOT)]
    wsl_f = [w.bitcast(F32) for w in wsl]
    scr = [sb("scr%d" % i, [128, 1024]) for i in range(NSCR)]
    small = [sb("small%d" % i, [128, 16]) for i in range(8)]
    uprev = sb("uprev", [128, 16])
    rd_t = sb("rd_t", [128, 512])
    c_cvec = sb("c_cvec", [128, KC * 2])
    c_sc = sb("c_sc", [128, KC * 2])
    c_bada = sb("c_bada", [128, L * 24])
    c_ng = sb("c_ng", [128, L * 8])
    c_qg = sb("c_qg", [128, L * 64])
    c_kg = sb("c_kg", [128, L * 64])
    c_qgp = sb("c_qgp", [128, L * 64])
    c_kgp = sb("c_kgp", [128, L * 64])
    cgsg = [sb("cgsg%d" % i, [128, 128]) for i in range(1)]
    c_convw = sb("c_convw", [128, L * 24])
    c_convb = sb("c_convb", [128, L * 8])
    c_bgate = sb("c_bgate", [128, L * 16])
    c_cos = sb("c_cos", [128, 16 * 64])
    c_sin = sb("c_sin", [128, 16 * 64])
    c_ident = sb("c_ident", [128, 128], BF16)
    c_ones = sb("c_ones", [128, 128])
    c_eps = sb("c_eps", [128, 1])
    c_mod = sb("c_mod", [128, L * 48])
    c_gs = sb("c_gs", [128, L * 16])
    c_negB = sb("c_negB", [128, L])
    psS = [nc.alloc_psum_tensor("psS%d" % i, [128, 1024], F32) for i in range(2)]
    ps1 = [nc.alloc_psum_tensor("ps%d" % i, [128, 512], F32) for i in range(4, 8)]
    ps = [psS[0][:, 0:512], psS[0][:, 512:1024], psS[1][:, 0:512], psS[1][:, 512:1024]] + [p[:, :] for p in ps1]
    ps_bf = {7: ps1[3].bitcast(BF16)}

    B = {}

    def buf(name):
        if name not in B:
            B[name] = Buf(name)
        return B[name]

    groups = [(0, CTX, True)] + [(CTX + G * j, G, False) for j in range(S // G)]
    NG = len(groups)
    GORDER = list(range(1, NG)) + [0]

    def xb(gi, kc):
        return buf("x_%d_%d" % (gi, kc))

    def xbs(gi):
        return [xb(gi, kc) for kc in range(KC)]

    b_rstd = [buf("rstd_%d" % gi) for gi in range(NG)]
    b_kT = [buf("kT_%d" % kt) for kt in range(18)]
    b_V = [buf("V_%d" % kt) for kt in range(18)]
    b_hT = [[buf("hT_%d_%d" % (i, kc)) for kc in range(KC)] for i in range(2)]
    b_g = [[buf("g_%d_%d" % (i, kc)) for kc in range(KC)] for i in range(3)]
    b_ws = [buf("ws_%d" % i) for i in range(NSLOT)]
    b_scr = [buf("scr_%d" % i) for i in range(NSCR)]
    b_small = [buf("small_%d" % i) for i in range(8)]
    b_uprev = [buf("uprev_%d" % c) for c in range(8)]
    b_rd = buf("rd_t")
    b_cgsg = [buf("cgsg_%d" % i) for i in range(1)]
    cgsg_ring = Ring([0])
    b_ps = [buf("ps_%d" % i) for i in range(8)]
    b_const = buf("const")
    b_ident = buf("ident")
    b_sc = buf("sc")
    b_mod = [buf("mod_%d" % l) for l in range(L)]
    b_negB = [buf("negB_%d" % l) for l in range(L)]
    b_misc = buf("misc_consts")
    b_out = buf("out")

    scr_ring = Ring(list(range(NSCR)))
    small_ring = Ring(list(range(8)))

    def scratch():
        i = scr_ring.next()
        return scr[i], b_scr[i]

    def smalls():
        i = small_ring.next()
        return small[i], b_small[i]

    consts = [(c_cvec, d_cvec), (c_bada, d_bada), (c_ng, d_ng), (c_qg, d_qg), (c_kg, d_kg), (c_qgp, d_qgp), (c_kgp, d_kgp),
              (c_convw, d_convw), (c_convb, d_convb), (c_bgate, d_bgate), (c_cos, d_cos),
              (c_sin, d_sin), (scr[0][:, 0:128], d_ident)]
    ctok = None
    for t_sb, t_dr in consts:
        ctok = sch.dma("sp", "const", (lambda e, o=t_sb, i=t_dr: e.dma_start(out=o[:, :], in_=i)), writes=[b_const, b_scr[0]])

    xT_dr = d_xT.rearrange("(kc p) t -> p kc t", p=128)
    for gi, (t0, n, _) in enumerate(groups):
        sch.dma("sp", "xld%d" % gi,
                (lambda e, t0=t0, n=n: e.dma_start(out=xT[:, :, t0:t0 + n], in_=xT_dr[:, :, t0:t0 + n])),
                writes=xbs(gi))

    sch.op("dve", [lambda e: e.memset(c_ones[:, :], 1.0)], writes=[b_misc])
    sch.op("dve", [lambda e: e.memset(c_eps[:, :], EPS)], writes=[b_misc])
    Vt5 = Vt[:, :, :].rearrange("p k (m b d) -> p k m b d", m=2, b=3, d=64)
    for kt in range(18):
        sch.op("dve", [lambda e, kt=kt: e.memset(Vt5[:, kt, :, 1, :], 1.0)], writes=[b_V[kt]])
    for g_ in range(4):
        sch.op("dve", [lambda e, g_=g_: e.memset(kT[:, g_, :], 0.0)], writes=b_kT)
    sch.op("dve", [lambda e: e.tensor_copy(out=c_ident[:, :], in_=scr[0][:, 0:128])], reads=[b_const, b_scr[0]], writes=[b_ident])
    sch.op("act", [lambda e: e.activation(out=c_sc[:, :], in_=c_cvec[:, :], func=AF.Silu)], reads=[b_const], writes=[b_sc])

    def group_take_order(l, gi):
        isctx = groups[gi][2]
        seq = [("cat", l, 1), ("cat", l, 2)]
        ada = {}
        if l + 1 < L and isctx:
            ada = {hk: hk - 1 for hk in range(1, 13)}
        for hk in range(1, 17):
            if hk in ada:
                seq.append(("ada", l + 1, ada[hk]))
            if hk % 2 == 1:
                seq.append(("cat", l, 5 + (hk - 1) // 2))
        seq += [("cat", l, 3), ("cat", l, 4)]
        seq += [("cat", l, b) for b in range(13, NBLK)]
        return seq

    wseq = []
    for fb in range(12):
        wseq.append(("ada", 0, fb))
    for l in range(L):
        wseq.append(("cat", l, 0))
        last = layers[l] == DEPTH - 1
        for gi in GORDER:
            if groups[gi][2] and last:
                continue
            wseq += group_take_order(l, gi)
    wstate = {"issued": 0}
    cast_done = set()
    cast_hist = []

    def emit_cast(l, blk, after=()):
        if (l, blk) in cast_done:
            return
        cast_done.add((l, blk))
        cast_hist.append(buf("wbf_%d_%d" % (l, blk)))
        if len(cast_hist) > 4:
            after = list(after) + [cast_hist[-5]]
        src = d_wcat[l, blk].rearrange("(kc p) n -> p kc n", p=128)
        dst = d_wbf[l, blk].rearrange("p (kc n) -> p kc n", kc=KC)
        sch.dma("pool", "cast_%d_%d" % (l, blk), (lambda e, o=dst, i_=src: e.dma_start(out=o, in_=i_)),
                reads=list(after), writes=[buf("wbf_%d_%d" % (l, blk))])

    cast_order = [0, 1, 2, 5, 6, 7, 8, 9, 10, 11, 12, 3, 4] + list(range(13, 23))

    def w_issue_upto(k):
        while wstate["issued"] < min(k, len(wseq)):
            i = wstate["issued"]
            kind, l, idx = wseq[i]
            s = i % NSLOT
            if kind == "ada":
                src = d_wada[l, :, idx * 256:(idx + 1) * 256].rearrange("(kc p) n -> p kc n", p=128)
                dst = wsl_f[s][:, :, :]
                sch.dma("sp", "wa%d" % s, (lambda e, o=dst, i_=src: e.dma_start(out=o, in_=i_)), writes=[b_ws[s]])
            else:
                emit_cast(l, idx)
                src = d_wbf[l, idx].rearrange("p (kc n) -> p kc n", kc=KC)
                dst = wsl[s][:, :, :]
                sch.dma("sp", "ws%d" % s, (lambda e, o=dst, i_=src: e.dma_start(out=o, in_=i_)),
                        reads=[buf("wbf_%d_%d" % (l, idx))], writes=[b_ws[s]])
            wstate["issued"] += 1

    wcur = {"i": 0}

    def w_take(kind, l, idx):
        i = wcur["i"]
        assert wseq[i] == (kind, l, idx), (wseq[i], kind, l, idx)
        assert i < wstate["issued"] or i < NSLOT or wrel["k"] >= i - NSLOT, ("slot still busy", i, wrel["k"])
        w_issue_upto(i + 1)
        wcur["i"] += 1
        return i % NSLOT, i

    wdone = set()
    wrel = {"k": -1}

    def w_done(i):
        wdone.add(i)
        while (wrel["k"] + 1) in wdone:
            wrel["k"] += 1
        w_issue_upto(wrel["k"] + NSLOT + 1)

    for blk in cast_order[:3]:
        emit_cast(0, blk)
    w_issue_upto(NSLOT)

    def mod_ap(l, f, col):
        o = l * 48 + f * 2 + col
        return c_mod[:, o:o + 1]

    def gs_ap(l, kc, col):
        o = l * 16 + kc * 2 + col
        return c_gs[:, o:o + 1]

    def mm_group(out_ap, pairs, bank, reads):
        n = len(pairs)
        fns = []
        for k, (lt, rh) in enumerate(pairs):
            fns.append(lambda e, lt=lt, rh=rh, k=k: e.matmul(out_ap, lhsT=lt, rhs=rh, start=(k == 0), stop=(k == n - 1)))
        sch.op("pe", fns, reads=reads, writes=[b_ps[bank]])

    def mm_list(out_ap, pairs, bank, reads):
        n = len(pairs)
        lst = []
        for k, (lt, rh) in enumerate(pairs):
            if k == n - 1:
                lst.append(lambda k=k, lt=lt, rh=rh: sch.op(
                    "pe", [lambda e: e.matmul(out_ap, lhsT=lt, rhs=rh, start=(k == 0), stop=True)],
                    reads=reads, writes=[b_ps[bank]]))
            else:
                lst.append(lambda k=k, lt=lt, rh=rh: sch.op_noinc(
                    "pe", (lambda e: e.matmul(out_ap, lhsT=lt, rhs=rh, start=(k == 0), stop=False)),
                    reads=reads, writes=[b_ps[bank]]))
        return lst

    sc3 = c_sc[:, :].rearrange("p (kc c) -> p kc c", c=2)

    def stage_mod_block(l, fb):
        bank = 7
        s_, wi = w_take("ada", l, fb)
        wv = wsl_f[s_]
        for j in range(2):
            mm_group(ps[bank][:, j * 2:j * 2 + 2],
                     [(wv[:, kc, j * 128:(j + 1) * 128], sc3[:, kc, :]) for kc in range(KC)],
                     bank, [b_ws[s_], b_sc])
        w_done(wi)
        f0 = fb * 2
        sch.op("dve", [lambda e: e.tensor_tensor(
            out=c_mod[:, l * 48 + f0 * 2:l * 48 + f0 * 2 + 4].rearrange("p (f c) -> p f c", c=2),
            in0=ps[bank][:, 0:4].rearrange("p (f c) -> p f c", c=2),
            in1=c_bada[:, l * 24 + f0:l * 24 + f0 + 2].unsqueeze(2).to_broadcast([128, 2, 2]),
            op=ALU.add)], reads=[b_ps[bank], b_const], writes=[b_mod[l]])

    def stage_mod_finish(l):
        sch.op("dve", [lambda e: e.scalar_tensor_tensor(
            out=c_gs[:, l * 16:(l + 1) * 16].rearrange("p (f c) -> p f c", c=2),
            in0=c_mod[:, l * 48 + 16:l * 48 + 32].rearrange("p (f c) -> p f c", c=2),
            scalar=1.0,
            in1=c_ng[:, l * 8:(l + 1) * 8].unsqueeze(2).to_broadcast([128, 8, 2]),
            op0=ALU.add, op1=ALU.mult)], reads=[b_mod[l], b_const], writes=[b_mod[l]])
        sm1, bsm1 = smalls()
        sch.op("dve", [lambda e: e.reduce_max(out=sm1[:, 0:1], in_=c_qg[:, l * 64:(l + 1) * 64], axis=AX.X,
                                              apply_absolute_value=True)], reads=[b_const], writes=[bsm1])
        sch.op("dve", [lambda e: e.reduce_max(out=sm1[:, 1:2], in_=c_kg[:, l * 64:(l + 1) * 64], axis=AX.X,
                                              apply_absolute_value=True)], reads=[b_const], writes=[bsm1])
        sch.op("dve", [lambda e: e.tensor_scalar(out=c_negB[:, l:l + 1], in0=sm1[:, 0:1], scalar1=sm1[:, 1:2],
                                                 scalar2=-8.0, op0=ALU.mult, op1=ALU.mult)],
               reads=[bsm1], writes=[b_negB[l]])


    STOP = stop
    def stage_norm(l, gi):
        t0, n, _ = groups[gi]
        bank = 6
        for kc in range(KC):
            sq, bsq = scratch()
            sch.op("act", [lambda e, sq=sq, kc=kc: e.activation(out=sq[:, 0:n], in_=xT[:, kc, t0:t0 + n], func=AF.Square)],
                   reads=[xb(gi, kc)], writes=[bsq])
            sch.op("pe", [lambda e, sq=sq, kc=kc: e.matmul(ps[bank][:, 0:n], lhsT=c_ones[:, :], rhs=sq[:, 0:n],
                                                           start=(kc == 0), stop=(kc == KC - 1))],
                   reads=[bsq, b_misc], writes=[b_ps[bank]])
        vt, bvt = scratch()
        sch.op("act", [lambda e: e.activation(out=vt[:, 0:n], in_=ps[bank][:, 0:n], func=AF.Ln, scale=1.0 / D, bias=c_eps[:, 0:1])],
               reads=[b_ps[bank], b_misc], writes=[bvt])
        sch.op("act", [lambda e: e.activation(out=rstd_all[:, t0:t0 + n], in_=vt[:, 0:n], func=AF.Exp, scale=-0.5)],
               reads=[bvt], writes=[b_rstd[gi]])

    def stage_hT(l, gi, hs, extra):
        t0, n, isctx = groups[gi]
        col = 1 if isctx else 0
        m = n + extra
        rd_extra = [b_rstd[gi + 1]] if extra else []
        for kc in range(KC):
            tmp, btmp = scratch()
            rds = [xb(gi, kc), b_rstd[gi]] + rd_extra + ([xb(gi + 1, kc)] if extra else [])
            sch.op("dve", [lambda e, tmp=tmp, kc=kc: e.tensor_tensor(out=tmp[:, 0:m], in0=xT[:, kc, t0:t0 + m],
                                                                    in1=rstd_all[:, t0:t0 + m], op=ALU.mult)],
                   reads=rds, writes=[btmp])
            sch.op("act", [lambda e, tmp=tmp, kc=kc: e.activation(out=hT[hs][:, kc, 0:m], in_=tmp[:, 0:m], func=AF.Identity,
                                                                  scale=gs_ap(l, kc, col), bias=mod_ap(l, kc, col))],
                   reads=[btmp, b_mod[l]], writes=[b_hT[hs][kc]])

    def rms_rope_a(src, bsrc, nh):
        w = nh * 64
        sq, bsq = scratch()
        sch.op("act", [lambda e: e.activation(out=sq[:, 0:w], in_=src[:, 0:w], func=AF.Square)],
               reads=[bsrc], writes=[bsq])
        return sq, bsq

    def rms_rope(l, src, bsrc, nh, gvec, tile_lat, perm, pre=None, gpvec=None):
        w = nh * 64
        sq, bsq = pre if pre is not None else rms_rope_a(src, bsrc, nh)
        ss, bss = smalls()
        sch.op("dve", [lambda e: e.reduce_sum(out=ss[:, 0:nh], in_=sq[:, 0:w].rearrange("p (h d) -> p h d", d=64), axis=AX.X)],
               reads=[bsq], writes=[bss])
        sch.op("act", [lambda e: e.activation(out=ss[:, 0:nh], in_=ss[:, 0:nh], func=AF.Ln, scale=1.0 / HD, bias=c_eps[:, 0:1])],
               reads=[bss, b_misc], writes=[bss])
        rr, brr = smalls()
        sch.op("act", [lambda e: e.activation(out=rr[:, 0:nh], in_=ss[:, 0:nh], func=AF.Exp, scale=-0.5)],
               reads=[bss], writes=[brr])
        src3 = src[:, 0:w].rearrange("p (h d) -> p h d", d=64)
        qn, bqn = scratch()
        qn3 = qn[:, 0:w].rearrange("p (h d) -> p h d", d=64)
        sch.op("dve", [lambda e: e.tensor_tensor(out=qn3, in0=src3, in1=rr[:, 0:nh].unsqueeze(2).to_broadcast([128, nh, 64]),
                                                 op=ALU.mult)], reads=[bsrc, brr], writes=[bqn])
        def out_views(dst_bf, a_in, b_in):
            if not perm:
                return [(dst_bf, a_in, b_in)]
            res = []
            for i in range(2):
                o = dst_bf[:, i * 512:(i + 1) * 512].rearrange("p (j r d) -> p r j d", j=4, r=2, d=64)
                a = a_in[:, i * 512:(i + 1) * 512].rearrange("p (r j d) -> p r j d", j=4, r=2, d=64)
                b = b_in[:, i * 512:(i + 1) * 512].rearrange("p (r j d) -> p r j d", j=4, r=2, d=64) if b_in is not None else None
                res.append((o, a, b))
            return res

        if tile_lat is None:
            dst, bdst = scratch()
            dst_bf = dst.bitcast(BF16)[:, 0:w]
            fns = []
            for o, a, _ in out_views(dst_bf, qn[:, 0:w], None):
                if perm:
                    gb = gvec.unsqueeze(1).unsqueeze(1).to_broadcast([128, 2, 4, 64])
                else:
                    o = o.rearrange("p (h d) -> p h d", d=64)
                    a = a.rearrange("p (h d) -> p h d", d=64)
                    gb = gvec.unsqueeze(1).to_broadcast([128, nh, 64])
                fns.append(lambda e, o=o, a=a, gb=gb: e.tensor_tensor(out=o, in0=a, in1=gb, op=ALU.mult))
            sch.op("dve", fns, reads=[bqn, b_const], writes=[bdst])
            return dst_bf, bdst
        ci = cgsg_ring.next()
        cg_t, bcg = cgsg[ci], b_cgsg[ci]
        sch.op("dve", [lambda e: e.tensor_tensor(out=cg_t[:, 0:64], in0=c_cos[:, tile_lat * 64:(tile_lat + 1) * 64], in1=gvec, op=ALU.mult),
                       lambda e: e.tensor_tensor(out=cg_t[:, 64:128], in0=c_sin[:, tile_lat * 64:(tile_lat + 1) * 64], in1=gpvec, op=ALU.mult)],
               reads=[b_const], writes=[bcg])
        cosv = cg_t[:, 0:64]
        sinv = cg_t[:, 64:128]
        t1, bt1 = scratch()
        t13 = t1[:, 0:w].rearrange("p (h d) -> p h d", d=64)
        sch.op("dve", [lambda e: e.tensor_tensor(out=t13, in0=qn3, in1=cosv.unsqueeze(1).to_broadcast([128, nh, 64]),
                                                 op=ALU.mult)], reads=[bqn, bcg], writes=[bt1])
        t2, bt2 = scratch()
        qn5 = qn[:, 0:w].rearrange("p (h a b d) -> p h a b d", a=2, b=2, d=16)
        t25 = t2[:, 0:w].rearrange("p (h a b d) -> p h a b d", a=2, b=2, d=16)
        sin4 = sinv.rearrange("p (a b d) -> p a b d", a=2, b=2, d=16)
        fns = []
        for hh in range(2):
            for bb in range(2):
                fns.append(lambda e, hh=hh, bb=bb: e.tensor_tensor(
                    out=t25[:, :, hh, bb, :], in0=qn5[:, :, hh, 1 - bb, :],
                    in1=sin4[:, hh, bb, :].unsqueeze(1).to_broadcast([128, nh, 16]), op=ALU.mult))
        sch.op("dve", fns, reads=[bqn, bcg], writes=[bt2])
        dst, bdst = scratch()
        dst_bf = dst.bitcast(BF16)[:, 0:w]
        fns = []
        for o, a, b in out_views(dst_bf, t1[:, 0:w], t2[:, 0:w]):
            fns.append(lambda e, o=o, a=a, b=b: e.tensor_tensor(out=o, in0=a, in1=b, op=ALU.add))
        sch.op("dve", fns, reads=[bt1, bt2], writes=[bdst])
        return dst_bf, bdst

    def pipeline(phases_per_tile):
        nt = len(phases_per_tile)
        phases_per_tile[0][0]()
        phases_per_tile[0][3]()
        phases_per_tile[0][1]()
        for t in range(nt):
            if t + 1 < nt:
                phases_per_tile[t + 1][0]()
                phases_per_tile[t + 1][3]()
            phases_per_tile[t][2]()
            if t + 1 < nt:
                phases_per_tile[t + 1][1]()

    def stage_kv(l, gi, hs, slot):
        t0, n, isctx = groups[gi]
        tiles = []
        for ti in range(n // 128):
            kt = t0 // 128 + ti
            st = {}

            def f1(ti=ti, kt=kt, st=st):
                bank = 4 + (kt % 2)
                mm_group(ps[bank][:, :],
                         [(hT[hs][:, kc, ti * 128:(ti + 1) * 128], wsl[slot][:, kc, :]) for kc in range(KC)],
                         bank, b_hT[hs] + [b_ws[slot]])
                kv, bkv = scratch()
                st["kv"], st["bkv"] = kv, bkv
                sch.op("act", [lambda e, kv=kv, bank=bank: e.activation(out=kv[:, 0:512], in_=ps[bank][:, :], func=AF.Copy)],
                       reads=[b_ps[bank]], writes=[bkv])

            def f2(ti=ti, kt=kt, st=st):
                kv, bkv = st["kv"], st["bkv"]
                sch.op("dve", [lambda e, kv=kv, kt=kt: e.tensor_copy(out=Vt5[:, kt, :, 0:3:2, :],
                                                                     in_=kv[:, 256:512].rearrange("p (m r d) -> p m r d", m=2, r=2, d=64))],
                       reads=[bkv], writes=[b_V[kt]])
                st["kr"], st["bkr"] = rms_rope(l, kv, bkv, NKV, c_kg[:, l * 64:(l + 1) * 64],
                                               None if isctx else (kt - CTX // 128), False, pre=st["pre"],
                                               gpvec=c_kgp[:, l * 64:(l + 1) * 64])

            def f2a(ti=ti, kt=kt, st=st):
                st["pre"] = rms_rope_a(st["kv"], st["bkv"], NKV)

            def f3(ti=ti, kt=kt, st=st):
                kr_bf, bkr = st["kr"], st["bkr"]
                tb = 7
                fns = []
                for j in range(2):
                    fns.append(lambda e, j=j, kr_bf=kr_bf: e.transpose(out=ps_bf[tb][:, j * 128:(j + 1) * 128],
                                                                       in_=kr_bf[:, j * 128:(j + 1) * 128], identity=c_ident[:, :]))
                sch.op("pe", fns, reads=[bkr, b_ident], writes=[b_ps[tb]])
                fns = []
                for r in range(2):
                    fns.append(lambda e, kt=kt, r=r: e.activation(
                        out=kT[r * 64:(r + 1) * 64, r:4:2, kt * 128:(kt + 1) * 128],
                        in_=ps_bf[tb][r * 64:(r + 1) * 64, 0:256].rearrange("p (j t) -> p j t", j=2), func=AF.Copy))
                sch.op("act", fns, reads=[b_ps[tb]], writes=[b_kT[kt]])

            tiles.append((f1, f2, f3, f2a))
        pipeline(tiles)

    def stage_q(l, gi, hs, s0, s1, qb_i):
        t0, n, isctx = groups[gi]
        qT = gbuf[qb_i]
        tiles = []
        for ti in range(n // 128):
            st = {}

            def f1(ti=ti, st=st):
                q_sb, bq = scratch()
                st["q"], st["bq"] = q_sb, bq
                for blk, sl in enumerate((s0, s1)):
                    bank = 4 + blk
                    mm_group(ps[bank][:, :],
                             [(hT[hs][:, kc, ti * 128:(ti + 1) * 128], wsl[sl][:, kc, :]) for kc in range(KC)],
                             bank, b_hT[hs] + [b_ws[sl]])
                    sch.op("act", [lambda e, blk=blk, bank=bank, q_sb=q_sb: e.activation(out=q_sb[:, blk * 512:(blk + 1) * 512],
                                                                                         in_=ps[bank][:, :], func=AF.Copy)],
                           reads=[b_ps[bank]], writes=[bq])

            def f2(ti=ti, st=st):
                tile_lat = None if isctx else (t0 - CTX) // 128 + ti
                st["qr"], st["bqr"] = rms_rope(l, st["q"], st["bq"], NH, c_qg[:, l * 64:(l + 1) * 64], tile_lat, True,
                                               pre=st["pre"], gpvec=c_qgp[:, l * 64:(l + 1) * 64])

            def f2a(ti=ti, st=st):
                st["pre"] = rms_rope_a(st["q"], st["bq"], NH)

            def f3(ti=ti, st=st):
                qr_bf, bqr = st["qr"], st["bqr"]
                tb = 7
                fns = []
                for c in range(8):
                    fns.append(lambda e, c=c, qr_bf=qr_bf: e.transpose(out=ps_bf[tb][:, c * 128:(c + 1) * 128],
                                                                       in_=qr_bf[:, c * 128:(c + 1) * 128], identity=c_ident[:, :]))
                sch.op("pe", fns, reads=[bqr, b_ident], writes=[b_ps[tb]])
                sch.op("act", [lambda e, ti=ti: e.activation(out=qT[:, :, ti * 128:(ti + 1) * 128],
                                                             in_=ps_bf[tb][:, :].rearrange("p (c t) -> p c t", c=8),
                                                             func=AF.Copy)], reads=[b_ps[tb]], writes=b_g[qb_i])

            tiles.append((f1, f2, f3, f2a))
        pipeline(tiles)

    def stage_att(l, gi, q_i, a_i, feeder=None):
        t0, n, isctx = groups[gi]
        qT = gbuf[q_i]
        ga = gbuf[a_i]
        kts = [0, 1] if isctx else list(range(18))
        pairs = [(kts[2 * p], kts[2 * p + 1]) for p in range(len(kts) // 2)]
        npair = len(pairs)
        iters = [(g, qb) for g in range(NKV) for qb in range(n // 128)]
        steps = [(it, p) for it in range(len(iters)) for p in range(npair)]

        def emit_S(k):
            it, p = steps[k]
            g, qb = iters[it]
            i = g // 2
            sp = k % 2
            fns = []
            for h, kt in enumerate(pairs[p]):
                fns.append(lambda e, kt=kt, h=h, sp=sp, g=g, i=i, qb=qb: e.matmul(
                    psS[sp][:, h * 512:(h + 1) * 512], lhsT=kT[:, g, kt * 128:(kt + 1) * 128],
                    rhs=qT[:, 4 * i:4 * i + 4, qb * 128:(qb + 1) * 128], start=True, stop=True))
            sch.op("pe", fns, reads=[b_kT[pairs[p][0]], b_kT[pairs[p][1]]] + b_g[q_i], writes=[b_ps[2 * sp], b_ps[2 * sp + 1]])

        emit_S(0)
        if len(steps) > 1:
            emit_S(1)
        for k, (it, p) in enumerate(steps):
            g, qb = iters[it]
            if feeder and p == 0:
                feeder.start(it + 1)
            r = g % 2
            voff = (g // 2) * 192 + r * 64
            nump = r * 64
            denp = (1 - r) * 64
            ob = 4 + (it % 2)
            sp = k % 2
            pt, bpt = scratch()
            pt_bf = pt.bitcast(BF16)
            sch.op("act", [lambda e, pt_bf=pt_bf, sp=sp: e.activation(
                out=pt_bf[:, 0:1024], in_=psS[sp][:, 0:1024], func=AF.Exp, scale=HD ** -0.5, bias=c_negB[:, l:l + 1])],
                reads=[b_ps[2 * sp], b_ps[2 * sp + 1], b_negB[l]], writes=[bpt])
            if k + 2 < len(steps):
                emit_S(k + 2)
            fns = []
            for h, kt in enumerate(pairs[p]):
                first = (p == 0 and h == 0)
                lastm = (p == npair - 1 and h == 1)
                fns.append(lambda e, kt=kt, h=h, pt_bf=pt_bf, ob=ob, voff=voff, first=first, lastm=lastm: e.matmul(
                    ps[ob][:, :], lhsT=Vt[:, kt, voff:voff + 128], rhs=pt_bf[:, h * 512:(h + 1) * 512],
                    start=first, stop=lastm))
            sch.op("pe", fns, reads=[bpt, b_V[pairs[p][0]], b_V[pairs[p][1]]], writes=[b_ps[ob]])
            if feeder:
                feeder.pump(2)
            if p == npair - 1:
                rd, brd = rd_t, b_rd
                sch.op("dve", [lambda e, ob=ob, rd=rd, nump=nump, denp=denp: e.reciprocal(
                    out=rd[nump:nump + 64, 0:512], in_=ps[ob][denp:denp + 64, :])], reads=[b_ps[ob]], writes=[brd])
                o4 = ps[ob][nump:nump + 64, :].rearrange("p (j q) -> p j q", j=4)
                r4 = rd[nump:nump + 64, 0:512].rearrange("p (j q) -> p j q", j=4)
                fns = []
                for par in range(2):
                    fns.append(lambda e, par=par, o4=o4, r4=r4, g=g, qb=qb: e.tensor_tensor(
                        out=ga[par * 64:(par + 1) * 64, 2 * g:2 * g + 2, qb * 128:(qb + 1) * 128],
                        in0=o4[:, par:4:2, :], in1=r4[:, par:4:2, :], op=ALU.mult))
                sch.op("dve", fns, reads=[b_ps[ob], brd], writes=[b_g[a_i][2 * g], b_g[a_i][2 * g + 1]])
                if feeder:
                    feeder.finish()

    def stage_za(l, gi, hs, s0, s1, a_i):
        t0, n, isctx = groups[gi]
        ga = gbuf[a_i]
        for c in range(8):
            sl = (s0, s1)[c // 4]
            cols = (c % 4) * 128
            bank = 4 + (c % 4)
            mm_group(ps[bank][:, 0:n], [(wsl[sl][:, kc, cols:cols + 128], hT[hs][:, kc, 0:n]) for kc in range(KC)],
                     bank, b_hT[hs] + [b_ws[sl]])
            sz, bsz = scratch()
            sch.op("act", [lambda e, sz=sz, bank=bank: e.activation(out=sz[:, 0:n], in_=ps[bank][:, 0:n], func=AF.Silu)],
                   reads=[b_ps[bank]], writes=[bsz])
            sch.op("dve", [lambda e, sz=sz, c=c: e.tensor_tensor(out=ga[:, c, 0:n], in0=ga[:, c, 0:n], in1=sz[:, 0:n],
                                                                 op=ALU.mult)],
                   reads=[bsz, b_g[a_i][c]], writes=[b_g[a_i][c]])

    def conv_half_a(l, gi, hs, c, slot, bk=(6, 7)):
        t0, n, isctx = groups[gi]
        ot = 1 - hs
        u = hT_f[ot][:, 0:514]
        y = hT_f[ot][:, 514:1026]
        bu = b_hT[ot]
        first = (gi == 1)
        lastg = (gi == NG - 1)
        if isctx:
            off, m, a = 0, n, 1
        else:
            off, m, a = 1, (n - 1 if lastg else n), 2
        mms = []
        for k, bank in ((0, bk[0]), (1, bk[1])):
            mms += mm_list(ps[bank][:, 0:m], [(wsl[slot][:, kc, k * 128:(k + 1) * 128], hT[hs][:, kc, off:off + m]) for kc in range(KC)],
                           bank, b_hT[hs] + [b_ws[slot]])
        return mms, (lambda: conv_evac_a(l, gi, hs, c, slot, bk))

    def conv_evac_a(l, gi, hs, c, slot, bk=(6, 7)):
        t0, n, isctx = groups[gi]
        ot = 1 - hs
        u = hT_f[ot][:, 0:514]
        y = hT_f[ot][:, 514:1026]
        bu = b_hT[ot]
        first = (gi == 1)
        lastg = (gi == NG - 1)
        if isctx:
            off, m, a = 0, n, 1
        else:
            off, m, a = 1, (n - 1 if lastg else n), 2
        xs, bxs = scratch()
        sch.op("act", [lambda e: e.activation(out=xs[:, 0:m], in_=ps[bk[0]][:, 0:m], func=AF.Copy)], reads=[b_ps[bk[0]]], writes=[bxs])
        sch.op("dve", [lambda e: e.tensor_tensor(out=u[:, a:a + m], in0=ps[bk[1]][:, 0:m], in1=xs[:, 0:m], op=ALU.mult)],
               reads=[b_ps[bk[1]], bxs], writes=bu)
        if isctx:
            sch.op("dve", [lambda e: e.memset(u[:, 0:1], 0.0), lambda e: e.memset(u[:, n + 1:n + 2], 0.0)], writes=bu)
        else:
            if first:
                for k, bank in ((0, bk[0]), (1, bk[1])):
                    mm_group(ps[bank][:, 0:1], [(wsl[slot][:, kc, k * 128:(k + 1) * 128], hT[hs][:, kc, 0:1]) for kc in range(KC)],
                             bank, b_hT[hs] + [b_ws[slot]])
                hx, bhx = smalls()
                sch.op("act", [lambda e: e.activation(out=hx[:, 0:1], in_=ps[bk[0]][:, 0:1], func=AF.Copy)], reads=[b_ps[bk[0]]], writes=[bhx])
                sch.op("dve", [lambda e: e.tensor_tensor(out=u[:, 1:2], in0=ps[bk[1]][:, 0:1], in1=hx[:, 0:1], op=ALU.mult)],
                       reads=[b_ps[bk[1]], bhx], writes=bu)
                sch.op("dve", [lambda e: e.memset(u[:, 0:1], 0.0)], writes=bu)
            else:
                sch.op("dve", [lambda e: e.tensor_copy(out=u[:, 0:2], in_=uprev[:, 2 * c:2 * c + 2])], reads=[b_uprev[c]], writes=bu)
            if lastg:
                sch.op("dve", [lambda e: e.memset(u[:, n + 1:n + 2], 0.0)], writes=bu)
            else:
                sch.op("dve", [lambda e: e.tensor_copy(out=uprev[:, 2 * c:2 * c + 2], in_=u[:, n:n + 2])], reads=bu, writes=[b_uprev[c]])
        cw = lambda k: c_convw[:, l * 24 + k * 8 + c:l * 24 + k * 8 + c + 1]
        sch.op("dve", [lambda e: e.tensor_scalar(out=y[:, 0:n], in0=u[:, 0:n], scalar1=cw(0), scalar2=None, op0=ALU.mult)],
               reads=bu + [b_const], writes=bu)
        sch.op("dve", [lambda e: e.scalar_tensor_tensor(out=y[:, 0:n], in0=u[:, 1:n + 1], scalar=cw(1), in1=y[:, 0:n],
                                                        op0=ALU.mult, op1=ALU.add)], reads=bu + [b_const], writes=bu)
        sch.op("dve", [lambda e: e.scalar_tensor_tensor(out=y[:, 0:n], in0=u[:, 2:n + 2], scalar=cw(2), in1=y[:, 0:n],
                                                        op0=ALU.mult, op1=ALU.add)], reads=bu + [b_const], writes=bu)

    def conv_half_b(l, gi, hs, c, slot, y_i, bk=(6, 7)):
        t0, n, isctx = groups[gi]
        ot = 1 - hs
        y = hT_f[ot][:, 514:1026]
        bu = b_hT[ot]
        yz = gbuf[y_i]
        mms = []
        for k, bank in ((2, bk[0]), (3, bk[1])):
            mms += mm_list(ps[bank][:, 0:n], [(wsl[slot][:, kc, k * 128:(k + 1) * 128], hT[hs][:, kc, 0:n]) for kc in range(KC)],
                           bank, b_hT[hs] + [b_ws[slot]])
        return mms, (lambda: conv_evac_b(l, gi, hs, c, slot, y_i, bk))

    def conv_evac_b(l, gi, hs, c, slot, y_i, bk=(6, 7)):
        t0, n, isctx = groups[gi]
        ot = 1 - hs
        y = hT_f[ot][:, 514:1026]
        bu = b_hT[ot]
        yz = gbuf[y_i]
        cb = c_convb[:, l * 8 + c:l * 8 + c + 1]
        sch.op("dve", [lambda e: e.scalar_tensor_tensor(out=y[:, 0:n], in0=y[:, 0:n], scalar=cb, in1=ps[bk[0]][:, 0:n],
                                                        op0=ALU.add, op1=ALU.mult)], reads=bu + [b_ps[bk[0]], b_const], writes=bu)
        tz, btz = scratch()
        sch.op("act", [lambda e: e.activation(out=tz[:, 0:n], in_=ps[bk[1]][:, 0:n], func=AF.Tanh, scale=0.5)],
               reads=[b_ps[bk[1]]], writes=[btz])
        sch.op("dve", [lambda e: e.scalar_tensor_tensor(out=tz[:, 0:n], in0=tz[:, 0:n], scalar=1.0, in1=ps[bk[1]][:, 0:n],
                                                        op0=ALU.add, op1=ALU.mult)], reads=[btz, b_ps[bk[1]]], writes=[btz])
        sch.op("dve", [lambda e: e.scalar_tensor_tensor(out=yz[:, c, 0:n], in0=y[:, 0:n], scalar=0.5, in1=tz[:, 0:n],
                                                        op0=ALU.mult, op1=ALU.mult)], reads=bu + [btz], writes=[b_g[y_i][c]])

    def stage_merge(l, gi, hs, j, slot, a_i, y_i, m_i):
        t0, n, isctx = groups[gi]
        pb = (j % 2) * 4
        srcs = [(gbuf[a_i], b_g[a_i]), (gbuf[y_i], b_g[y_i]), (hT[hs], b_hT[hs]), (hT[hs], b_hT[hs])]
        for k in range(4):
            bank = pb + k
            src, bsrc = srcs[k]
            mm_group(ps[bank][:, 0:n], [(wsl[slot][:, kc, k * 128:(k + 1) * 128], src[:, kc, 0:n]) for kc in range(KC)],
                     bank, bsrc + [b_ws[slot]])
        s0, bs0 = scratch()
        s1, bs1 = scratch()
        for k, (sx, bsx) in enumerate(((s0, bs0), (s1, bs1))):
            bg = c_bgate[:, l * 16 + k * 8 + j:l * 16 + k * 8 + j + 1]
            sch.op("act", [lambda e, sx=sx, k=k, bg=bg: e.activation(out=sx[:, 0:n], in_=ps[pb + 2 + k][:, 0:n],
                                                                     func=AF.Sigmoid, bias=bg)],
                   reads=[b_ps[pb + 2 + k], b_const], writes=[bsx])
        sch.op("dve", [lambda e: e.tensor_tensor(out=s0[:, 0:n], in0=s0[:, 0:n], in1=ps[pb][:, 0:n], op=ALU.mult)],
               reads=[bs0, b_ps[pb]], writes=[bs0])
        sch.op("dve", [lambda e: e.tensor_tensor(out=s1[:, 0:n], in0=s1[:, 0:n], in1=ps[pb + 1][:, 0:n], op=ALU.mult)],
               reads=[bs1, b_ps[pb + 1]], writes=[bs1])
        sch.op("dve", [lambda e: e.tensor_tensor(out=gbuf[m_i][:, j, 0:n], in0=s0[:, 0:n], in1=s1[:, 0:n], op=ALU.add)],
               reads=[bs0, bs1], writes=[b_g[m_i][j]])

    def stage_out(l, gi, s0, s1, m_i):
        t0, n, isctx = groups[gi]
        col = 1 if isctx else 0
        mg = gbuf[m_i]
        for j in range(8):
            sl = (s0, s1)[j // 4]
            cols = (j % 4) * 128
            bank = j % 4
            mm_group(ps[bank][:, 0:n], [(wsl[sl][:, kc, cols:cols + 128], mg[:, kc, 0:n]) for kc in range(KC)],
                     bank, b_g[m_i] + [b_ws[sl]])
            sch.op("dve", [lambda e, j=j, bank=bank: e.scalar_tensor_tensor(
                out=xT[:, j, t0:t0 + n], in0=ps[bank][:, 0:n], scalar=mod_ap(l, 16 + j, col), in1=xT[:, j, t0:t0 + n],
                op0=ALU.mult, op1=ALU.add)], reads=[b_ps[bank], xb(gi, j), b_mod[l]], writes=[xb(gi, j)])

    outT_dr = d_out.rearrange("(kc p) t -> p kc t", p=128)

    def store_group(gi):
        t0, n, _ = groups[gi]
        sch.dma("sp", "out", (lambda e: e.dma_start(out=outT_dr[:, :, t0:t0 + n], in_=xT[:, :, t0:t0 + n])),
                reads=xbs(gi), writes=[b_out])

    norm_done = set()
    for gi in range(NG):
        stage_norm(0, gi)
        norm_done.add((0, gi))
    for fb in range(12):
        stage_mod_block(0, fb)
    stage_mod_finish(0)

    hring = Ring([0, 1])
    pre_hT = {}
    for l in range(L):
        last = layers[l] == DEPTH - 1
        final = (l == L - 1)
        skv, wkv = w_take("cat", l, 0)
        sch.mark("L%d prepass" % l)
        for gi in range(NG):
            hs = hring.next()
            if (l, gi) not in norm_done:
                stage_norm(l, gi)
            stage_hT(l, gi, hs, 0)
            stage_kv(l, gi, hs, skv)
        w_done(wkv)
        if l == 0:
            for blk in cast_order[3:]:
                emit_cast(0, blk, after=[b_kT[17]])
        if STOP == "pre":
            break
        for gi in GORDER:
            t0, n, isctx = groups[gi]
            if isctx and last:
                if final:
                    store_group(gi)
                continue
            has_right = (not isctx) and gi < NG - 1
            has_left = (not isctx) and gi > 1
            sch.mark("L%d g%d hT+q" % (l, gi))
            if pre_hT.get((l, gi)) is None:
                hs = hring.next()
                stage_hT(l, gi, hs, 1 if has_right else 0)
            else:
                hs = pre_hT[(l, gi)]
            q_i, a_i, y_i = 0, 1, 2
            m_i = 0
            sq0, wq0 = w_take("cat", l, 1)
            sq1, wq1 = w_take("cat", l, 2)
            stage_q(l, gi, hs, sq0, sq1, q_i)
            w_done(wq0)
            w_done(wq1)
            if STOP == "q":
                break
            if l + 1 < L and gi != 1:
                per = (NBLK + (NG - 1) - 1) // (NG - 1)
                pos_c = GORDER.index(gi) - 1
                for blk in cast_order[pos_c * per:(pos_c + 1) * per]:
                    emit_cast(l + 1, blk)
            sch.mark("L%d g%d att" % (l, gi))
            ada = {}
            if l + 1 < L and isctx:
                ada = {hk: hk - 1 for hk in range(1, 13)}

            class Feeder:
                def __init__(self):
                    self.mms = []
                    self.evac = None
                    self.slot = None
                    self.wi = None

                def start(self, hk, l=l, gi=gi, hs=hs, y_i=y_i, ada=ada, isctx=isctx):
                    if hk in ada:
                        stage_mod_block(l + 1, ada[hk])
                    c = (hk - 1) // 2
                    bk = ((6, 5) if hk % 2 == 1 else (4, 3)) if isctx else (6, 7)
                    if hk % 2 == 1:
                        self.slot, self.wi = w_take("cat", l, 5 + c)
                        self.mms, self.evac = conv_half_a(l, gi, hs, c, self.slot, bk)
                        self.release = None
                    else:
                        self.mms, self.evac = conv_half_b(l, gi, hs, c, self.slot, y_i, bk)
                        self.release = self.wi

                def pump(self, k):
                    for _ in range(k):
                        if self.mms:
                            self.mms.pop(0)()
                    if not self.mms and self.evac:
                        ev, self.evac = self.evac, None
                        if self.release is not None:
                            w_done(self.release)
                        ev()

                def finish(self):
                    self.pump(10 ** 6)

            fd = Feeder()
            if isctx:
                stage_att(l, gi, q_i, a_i, None)
                for hk in range(1, 17):
                    fd.start(hk)
                    fd.finish()
            else:
                stage_att(l, gi, q_i, a_i, fd)
            if l + 1 < L and isctx:
                stage_mod_finish(l + 1)
            sch.mark("L%d g%d za" % (l, gi))
            sz0, wz0 = w_take("cat", l, 3)
            sz1, wz1 = w_take("cat", l, 4)
            stage_za(l, gi, hs, sz0, sz1, a_i)
            w_done(wz0)
            w_done(wz1)
            sch.mark("L%d g%d merge" % (l, gi))
            for j in range(8):
                sm_, wm_ = w_take("cat", l, 13 + j)
                stage_merge(l, gi, hs, j, sm_, a_i, y_i, m_i)
                w_done(wm_)
            order_l = [g_ for g_ in GORDER if not (groups[g_][2] and last)]
            pos = order_l.index(gi)
            if pos + 1 < len(order_l):
                gn = order_l[pos + 1]
                hs_n = hring.next()
                assert hs_n == 1 - hs
                stage_hT(l, gn, hs_n, 1 if ((not groups[gn][2]) and gn < NG - 1) else 0)
                pre_hT[(l, gn)] = hs_n
            sch.mark("L%d g%d out" % (l, gi))
            so0, wo0 = w_take("cat", l, 21)
            so1, wo1 = w_take("cat", l, 22)
            stage_out(l, gi, so0, so1, m_i)
            w_done(wo0)
            w_done(wo1)
            if l + 1 < L:
                stage_norm(l + 1, gi)
                norm_done.add((l + 1, gi))
            if final:
                store_group(gi)
    if STOP is None:
        assert wcur["i"] == len(wseq), (wcur["i"], len(wseq))
    else:
        for gi in range(NG):
            store_group(gi)
    sch.wait_all("sp", [b_out.w])

    sch.mark("end")
    with nc.Block() as block:
        sch.replay(block)
    nc._stage_marks = sch.marks
    return nc


def _rope_tables():
    t = np.arange(S)
    row = (t // 64).astype(np.float32)
    col = (t % 64).astype(np.float32)
    half = HD // 2
    inv = (np.float32(10000.0) ** (-np.arange(0, half, 2, dtype=np.float32) / np.float32(half))).astype(np.float32)
    ang_r = row[:, None] * inv[None, :]
    ang_c = col[:, None] * inv[None, :]
    ang = np.concatenate([ang_r, ang_r, ang_c, ang_c], axis=-1).astype(np.float32)
    cos = np.cos(ang).astype(np.float32)
    sin = np.sin(ang).astype(np.float32)
    sign = np.concatenate([-np.ones(16), np.ones(16), -np.ones(16), np.ones(16)]).astype(np.float32)
    sinS = sin * sign[None, :]

    def lay(a):
        return np.ascontiguousarray(a.reshape(16, 128, 64).transpose(1, 0, 2).reshape(128, 16 * 64))

    return lay(cos), lay(sinS)


def _pvec(v):
    v = np.asarray(v, dtype=np.float32)
    lead = v.shape[:-1]
    n = v.shape[-1] // 128
    a = v.reshape(lead + (n, 128))
    a = np.moveaxis(a, -1, 0)
    return np.ascontiguousarray(a.reshape(128, -1))


def _wcat(w_in, w_attn_br, w_conv_br, w_out):
    blocks = []
    q = w_in[:, 0:1024]
    k = w_in[:, 1024:1280]
    v = w_in[:, 1280:1536]
    za = w_in[:, 1536:2560]
    xc = w_in[:, 2560:3584]
    bc = w_in[:, 3584:4608]
    cc = w_in[:, 4608:5632]
    zc = w_in[:, 5632:6656]
    gl0 = w_in[:, 6656:7680]
    gl1 = w_in[:, 7680:8704]
    blocks.append(np.concatenate([k, v], axis=1))
    blocks += [q[:, 0:512], q[:, 512:1024], za[:, 0:512], za[:, 512:1024]]
    for c in range(8):
        s = slice(c * 128, (c + 1) * 128)
        blocks.append(np.concatenate([xc[:, s], cc[:, s], bc[:, s], zc[:, s]], axis=1))
    for j in range(8):
        s = slice(j * 128, (j + 1) * 128)
        blocks.append(np.concatenate([w_attn_br[:, s], w_conv_br[:, s], gl0[:, s], gl1[:, s]], axis=1))
    blocks += [w_out[:, 0:512], w_out[:, 512:1024]]
    return np.ascontiguousarray(np.stack(blocks, axis=0), dtype=np.float32)


_SWAP = np.concatenate([np.arange(16, 32), np.arange(0, 16), np.arange(48, 64), np.arange(32, 48)])

_PROG_CACHE = {}


def _get_prog(layers):
    key = tuple(layers)
    if key not in _PROG_CACHE:
        _PROG_CACHE[key] = build_program(list(layers))
    return _PROG_CACHE[key]


def _run_layers(layers, xall_T, inputs):
    f = lambda a: np.asarray(a, dtype=np.float32)
    L = len(layers)
    li = list(layers)
    shared = {
        "wada": np.ascontiguousarray(f(inputs["w_ada"])[li]),
        "bada": np.concatenate([_pvec(f(inputs["b_ada"])[l]) for l in li], axis=1),
        "ng": np.concatenate([_pvec(f(inputs["norm_g"])[l]) for l in li], axis=1),
        "wcat": np.stack([_wcat(f(inputs["w_in"])[l], f(inputs["w_attn_br"])[l], f(inputs["w_conv_br"])[l],
                                f(inputs["w_out"])[l]) for l in li], axis=0),
        "qg": np.concatenate([np.tile(f(inputs["q_norm_g"])[l][None, :], (128, 1)) for l in li], axis=1),
        "kg": np.concatenate([np.tile(f(inputs["k_norm_g"])[l][None, :], (128, 1)) for l in li], axis=1),
        "qgp": np.concatenate([np.tile(f(inputs["q_norm_g"])[l][_SWAP][None, :], (128, 1)) for l in li], axis=1),
        "kgp": np.concatenate([np.tile(f(inputs["k_norm_g"])[l][_SWAP][None, :], (128, 1)) for l in li], axis=1),
        "convw": np.concatenate([_pvec(f(inputs["conv_w"])[l]) for l in li], axis=1),
        "convb": np.concatenate([_pvec(f(inputs["conv_b"])[l]) for l in li], axis=1),
        "bgate": np.concatenate([_pvec(f(inputs["b_gate"])[l]) for l in li], axis=1),
        "ident": np.eye(128, dtype=np.float32),
    }
    cosT, sinT = _rope_tables()
    shared["cosT"] = cosT
    shared["sinT"] = sinT
    shared = {k: np.ascontiguousarray(v, dtype=np.float32) for k, v in shared.items()}
    c = f(inputs["c"])
    c_ctx = f(inputs["c_ctx"])
    in_maps = []
    for b in range(N_CORES):
        cv = np.stack([c[b].reshape(KC, 128).T, c_ctx.reshape(KC, 128).T], axis=-1)
        m = dict(shared)
        m["xT"] = np.ascontiguousarray(xall_T[b], dtype=np.float32)
        m["cvec"] = np.ascontiguousarray(cv.reshape(128, KC * 2), dtype=np.float32)
        in_maps.append(m)
    nc = _get_prog(layers)
    res = run_bass_kernel_spmd(nc, in_maps, core_ids=list(range(N_CORES)))
    return [np.asarray(r["outT"]) for r in res.results]


FUSED = True


def kernel(x, c, ctx, c_ctx, norm_g, w_ada, b_ada, w_in, q_norm_g, k_norm_g, conv_w, conv_b,
           w_attn_br, w_conv_br, b_gate, w_out):
    inputs = dict(x=x, c=c, ctx=ctx, c_ctx=c_ctx, norm_g=norm_g, w_ada=w_ada, b_ada=b_ada, w_in=w_in,
                  q_norm_g=q_norm_g, k_norm_g=k_norm_g, conv_w=conv_w, conv_b=conv_b, w_attn_br=w_attn_br,
                  w_conv_br=w_conv_br, b_gate=b_gate, w_out=w_out)
    x = np.asarray(x, dtype=np.float32)
    ctx = np.asarray(ctx, dtype=np.float32)
    xall_T = [np.ascontiguousarray(np.concatenate([ctx[b], x[b]], axis=0).T) for b in range(N_CORES)]
    if FUSED:
        outs = _run_layers([0, 1], xall_T, inputs)
    else:
        mid = _run_layers([0], xall_T, inputs)
        outs = _run_layers([1], mid, inputs)
    out = np.stack([o[:, CTX:].T for o in outs], axis=0)
    return np.ascontiguousarray(out, dtype=np.float32)
```

```python
import numpy as np
import concourse.bass as bass
import concourse.mybir as mybir
from concourse.bass_utils import run_bass_kernel_spmd

F32 = mybir.dt.float32
BF16 = mybir.dt.bfloat16
ALU = mybir.AluOpType
AF = mybir.ActivationFunctionType
AX = mybir.AxisListType

D = 1024
S = 2048
CTX = 256
T = S + CTX
DEPTH = 2
HD = 64
NH = 16
NKV = 4
KC = 8
NBLK = 23
EPS = 1e-6
G = 512
NSLOT = 3
NSCR = 4
N_CORES = 8


class Buf:
    __slots__ = ("name", "w", "r")

    def __init__(self, name):
        self.name = name
        self.w = None
        self.r = {}


class Sched:
    ENGS = ("pe", "act", "dve", "pool", "sp")

    def __init__(self, nc):
        self.nc = nc
        self.streams = {e: [] for e in self.ENGS}
        self.esem = {e: nc.alloc_semaphore("es_" + e) for e in ("pe", "act", "dve", "pool")}
        self.ecnt = {e: 0 for e in self.esem}
        self.waited = {e: {} for e in self.ENGS}
        self.dsem = {}
        self.dcnt = {}
        self.nops = 0
        self.npe = 0
        self.marks = []

    def _deps(self, e, reads, writes):
        need = {}

        def add(tok):
            if tok is None:
                return
            s, v = tok
            if need.get(s, 0) < v:
                need[s] = v

        for b in reads:
            add(b.w)
        for b in writes:
            add(b.w)
            for s, v in b.r.items():
                add((s, v))
        own = self.esem.get(e)
        out = []
        wd = self.waited[e]
        for s, v in need.items():
            if e == "pe" and s is own:
                continue
            if wd.get(s, 0) < v:
                wd[s] = v
                out.append((s, v))
        return out

    def _mark(self, tok, reads, writes):
        s, v = tok
        for b in reads:
            if b.r.get(s, 0) < v:
                b.r[s] = v
        for b in writes:
            b.w = tok
            b.r = {}

    def mark(self, name):
        self.marks.append((name, self.npe))

    def op(self, e, fns, reads=(), writes=()):
        if e == "pe":
            self.npe += len(fns)
        waits = self._deps(e, reads, writes)
        sem = self.esem[e]
        self.ecnt[e] += 1
        tok = (sem, self.ecnt[e])

        def emit(eng, waits=waits, fns=fns, sem=sem):
            for s, v in waits:
                eng.wait_ge(s, v)
            for f in fns[:-1]:
                f(eng)
            fns[-1](eng).then_inc(sem, 1)

        self.streams[e].append(emit)
        self._mark(tok, reads, writes)
        self.nops += 1
        return tok

    def dma(self, q, key, fn, reads=(), writes=()):
        if key not in self.dsem:
            self.dsem[key] = self.nc.alloc_semaphore("ds_" + key)
            self.dcnt[key] = 0
        waits = self._deps(q, reads, writes)
        sem = self.dsem[key]
        self.dcnt[key] += 16
        tok = (sem, self.dcnt[key])

        def emit(eng, waits=waits, fn=fn, sem=sem):
            for s, v in waits:
                eng.wait_ge(s, v)
            fn(eng).then_inc(sem, 16)

        self.streams[q].append(emit)
        self._mark(tok, reads, writes)
        return tok

    def wait_all(self, q, toks):
        def emit(eng, toks=toks):
            for s, v in toks:
                eng.wait_ge(s, v)

        self.streams[q].append(emit)

    def replay(self, block):
        st = self.streams

        @block.sync
        def _(e):
            for f in st["sp"]:
                f(e)

        @block.gpsimd
        def _(e):
            for f in st["pool"]:
                f(e)

        @block.tensor
        def _(e):
            for f in st["pe"]:
                f(e)

        @block.vector
        def _(e):
            for f in st["dve"]:
                f(e)

        @block.scalar
        def _(e):
            for f in st["act"]:
                f(e)


class Ring:
    def __init__(self, items):
        self.items = items
        self.i = 0

    def next(self):
        it = self.items[self.i % len(self.items)]
        self.i += 1
        return it


def build_program(layers, stop=None):
    L = len(layers)
    nc = bass.Bass("TRN2", target_bir_lowering=False)
    sch = Sched(nc)

    def dram(name, shape, dt=F32, kind="ExternalInput"):
        return nc.dram_tensor(name, list(shape), dt, kind=kind).ap()

    d_xT = dram("xT", [D, T])
    d_cvec = dram("cvec", [128, KC * 2])
    d_wada = dram("wada", [L, D, 3 * D])
    d_bada = dram("bada", [128, L * 24])
    d_ng = dram("ng", [128, L * 8])
    d_wcat = dram("wcat", [L, NBLK, D, 512])
    d_qg = dram("qg", [128, L * 64])
    d_kg = dram("kg", [128, L * 64])
    d_qgp = dram("qgp", [128, L * 64])
    d_kgp = dram("kgp", [128, L * 64])
    d_convw = dram("convw", [128, L * 3 * 8])
    d_convb = dram("convb", [128, L * 8])
    d_bgate = dram("bgate", [128, L * 2 * 8])
    d_cos = dram("cosT", [128, 16 * 64])
    d_sin = dram("sinT", [128, 16 * 64])
    d_ident = dram("ident", [128, 128])
    d_out = dram("outT", [D, T], kind="ExternalOutput")
    d_wbf = nc.dram_tensor("wbf", [L, NBLK, 128, KC * 512], BF16, kind="Internal").ap()

    def sb(name, shape, dt=F32):
        return nc.alloc_sbuf_tensor(name, list(shape), dt)

    xT = sb("xT_sb", [128, KC, T])
    rstd_all = sb("rstd_all", [128, T])
    kT = sb("kT_sb", [128, 4, T], BF16)
    Vt = sb("Vt_sb", [128, 18, 384], BF16)
    hT_raw = [sb("hT%d" % i, [128, KC * (G + 1) + 0], BF16) for i in range(2)]
    hT = [h[:, :].rearrange("p (kc t) -> p kc t", kc=KC) for h in hT_raw]
    hT_f = [h.bitcast(F32) for h in hT_raw]
    gbuf = [sb("gbuf%d" % i, [128, KC, G], BF16) for i in range(3)]
    wsl = [sb("wslot%d" % i, [128, KC, 512], BF16) for i in range(NSLOT)]
    wsl_f = [w.bitcast(F32) for w in wsl]
    scr = [sb("scr%d" % i, [128, 1024]) for i in range(NSCR)]
    small = [sb("small%d" % i, [128, 16]) for i in range(8)]
    uprev = sb("uprev", [128, 16])
    rd_t = sb("rd_t", [128, 512])
    c_cvec = sb("c_cvec", [128, KC * 2])
    c_sc = sb("c_sc", [128, KC * 2])
    c_bada = sb("c_bada", [128, L * 24])
    c_ng = sb("c_ng", [128, L * 8])
    c_qg = sb("c_qg", [128, L * 64])
    c_kg = sb("c_kg", [128, L * 64])
    c_qgp = sb("c_qgp", [128, L * 64])
    c_kgp = sb("c_kgp", [128, L * 64])
    cgsg = [sb("cgsg%d" % i, [128, 128]) for i in range(1)]
    c_convw = sb("c_convw", [128, L * 24])
    c_convb = sb("c_convb", [128, L * 8])
    c_bgate = sb("c_bgate", [128, L * 16])
    c_cos = sb("c_cos", [128, 16 * 64])
    c_sin = sb("c_sin", [128, 16 * 64])
    c_ident = sb("c_ident", [128, 128], BF16)
    c_ones = sb("c_ones", [128, 128])
    c_eps = sb("c_eps", [128, 1])
    c_mod = sb("c_mod", [128, L * 48])
    c_gs = sb("c_gs", [128, L * 16])
    c_negB = sb("c_negB", [128, L])
    psS = [nc.alloc_psum_tensor("psS%d" % i, [128, 1024], F32) for i in range(2)]
    ps1 = [nc.alloc_psum_tensor("ps%d" % i, [128, 512], F32) for i in range(4, 8)]
    ps = [psS[0][:, 0:512], psS[0][:, 512:1024], psS[1][:, 0:512], psS[1][:, 512:1024]] + [p[:, :] for p in ps1]
    ps_bf = {7: ps1[3].bitcast(BF16)}

    B = {}

    def buf(name):
        if name not in B:
            B[name] = Buf(name)
        return B[name]

    groups = [(0, CTX, True)] + [(CTX + G * j, G, False) for j in range(S // G)]
    NG = len(groups)
    GORDER = list(range(1, NG)) + [0]

    def xb(gi, kc):
        return buf("x_%d_%d" % (gi, kc))

    def xbs(gi):
        return [xb(gi, kc) for kc in range(KC)]

    b_rstd = [buf("rstd_%d" % gi) for gi in range(NG)]
    b_kT = [buf("kT_%d" % kt) for kt in range(18)]
    b_V = [buf("V_%d" % kt) for kt in range(18)]
    b_hT = [[buf("hT_%d_%d" % (i, kc)) for kc in range(KC)] for i in range(2)]
    b_g = [[buf("g_%d_%d" % (i, kc)) for kc in range(KC)] for i in range(3)]
    b_ws = [buf("ws_%d" % i) for i in range(NSLOT)]
    b_scr = [buf("scr_%d" % i) for i in range(NSCR)]
    b_small = [buf("small_%d" % i) for i in range(8)]
    b_uprev = [buf("uprev_%d" % c) for c in range(8)]
    b_rd = buf("rd_t")
    b_cgsg = [buf("cgsg_%d" % i) for i in range(1)]
    cgsg_ring = Ring([0])
    b_ps = [buf("ps_%d" % i) for i in range(8)]
    b_const = buf("const")
    b_ident = buf("ident")
    b_sc = buf("sc")
    b_mod = [buf("mod_%d" % l) for l in range(L)]
    b_negB = [buf("negB_%d" % l) for l in range(L)]
    b_misc = buf("misc_consts")
    b_out = buf("out")

    scr_ring = Ring(list(range(NSCR)))
    small_ring = Ring(list(range(8)))

    def scratch():
        i = scr_ring.next()
        return scr[i], b_scr[i]

    def smalls():
        i = small_ring.next()
        return small[i], b_small[i]

    consts = [(c_cvec, d_cvec), (c_bada, d_bada), (c_ng, d_ng), (c_qg, d_qg), (c_kg, d_kg), (c_qgp, d_qgp), (c_kgp, d_kgp),
              (c_convw, d_convw), (c_convb, d_convb), (c_bgate, d_bgate), (c_cos, d_cos),
              (c_sin, d_sin), (scr[0][:, 0:128], d_ident)]
    ctok = None
    for t_sb, t_dr in consts:
        ctok = sch.dma("sp", "const", (lambda e, o=t_sb, i=t_dr: e.dma_start(out=o[:, :], in_=i)), writes=[b_const, b_scr[0]])

    xT_dr = d_xT.rearrange("(kc p) t -> p kc t", p=128)
    for gi, (t0, n, _) in enumerate(groups):
        sch.dma("sp", "xld%d" % gi,
                (lambda e, t0=t0, n=n: e.dma_start(out=xT[:, :, t0:t0 + n], in_=xT_dr[:, :, t0:t0 + n])),
                writes=xbs(gi))

    sch.op("dve", [lambda e: e.memset(c_ones[:, :], 1.0)], writes=[b_misc])
    sch.op("dve", [lambda e: e.memset(c_eps[:, :], EPS)], writes=[b_misc])
    Vt5 = Vt[:, :, :].rearrange("p k (m b d) -> p k m b d", m=2, b=3, d=64)
    for kt in range(18):
        sch.op("dve", [lambda e, kt=kt: e.memset(Vt5[:, kt, :, 1, :], 1.0)], writes=[b_V[kt]])
    for g_ in range(4):
        sch.op("dve", [lambda e, g_=g_: e.memset(kT[:, g_, :], 0.0)], writes=b_kT)
    sch.op("dve", [lambda e: e.tensor_copy(out=c_ident[:, :], in_=scr[0][:, 0:128])], reads=[b_const, b_scr[0]], writes=[b_ident])
    sch.op("act", [lambda e: e.activation(out=c_sc[:, :], in_=c_cvec[:, :], func=AF.Silu)], reads=[b_const], writes=[b_sc])

    def group_take_order(l, gi):
        isctx = groups[gi][2]
        seq = [("cat", l, 1), ("cat", l, 2)]
        ada = {}
        if l + 1 < L and isctx:
            ada = {hk: hk - 1 for hk in range(1, 13)}
        for hk in range(1, 17):
            if hk in ada:
                seq.append(("ada", l + 1, ada[hk]))
            if hk % 2 == 1:
                seq.append(("cat", l, 5 + (hk - 1) // 2))
        seq += [("cat", l, 3), ("cat", l, 4)]
        seq += [("cat", l, b) for b in range(13, NBLK)]
        return seq

    wseq = []
    for fb in range(12):
        wseq.append(("ada", 0, fb))
    for l in range(L):
        wseq.append(("cat", l, 0))
        last = layers[l] == DEPTH - 1
        for gi in GORDER:
            if groups[gi][2] and last:
                continue
            wseq += group_take_order(l, gi)
    wstate = {"issued": 0}
    cast_done = set()
    cast_hist = []

    def emit_cast(l, blk, after=()):
        if (l, blk) in cast_done:
            return
        cast_done.add((l, blk))
        cast_hist.append(buf("wbf_%d_%d" % (l, blk)))
        if len(cast_hist) > 4:
            after = list(after) + [cast_hist[-5]]
        src = d_wcat[l, blk].rearrange("(kc p) n -> p kc n", p=128)
        dst = d_wbf[l, blk].rearrange("p (kc n) -> p kc n", kc=KC)
        sch.dma("pool", "cast_%d_%d" % (l, blk), (lambda e, o=dst, i_=src: e.dma_start(out=o, in_=i_)),
                reads=list(after), writes=[buf("wbf_%d_%d" % (l, blk))])

    cast_order = [0, 1, 2, 5, 6, 7, 8, 9, 10, 11, 12, 3, 4] + list(range(13, 23))

    def w_issue_upto(k):
        while wstate["issued"] < min(k, len(wseq)):
            i = wstate["issued"]
            kind, l, idx = wseq[i]
            s = i % NSLOT
            if kind == "ada":
                src = d_wada[l, :, idx * 256:(idx + 1) * 256].rearrange("(kc p) n -> p kc n", p=128)
                dst = wsl_f[s][:, :, :]
                sch.dma("sp", "wa%d" % s, (lambda e, o=dst, i_=src: e.dma_start(out=o, in_=i_)), writes=[b_ws[s]])
            else:
                emit_cast(l, idx)
                src = d_wbf[l, idx].rearrange("p (kc n) -> p kc n", kc=KC)
                dst = wsl[s][:, :, :]
                sch.dma("sp", "ws%d" % s, (lambda e, o=dst, i_=src: e.dma_start(out=o, in_=i_)),
                        reads=[buf("wbf_%d_%d" % (l, idx))], writes=[b_ws[s]])
            wstate["issued"] += 1

    wcur = {"i": 0}

    def w_take(kind, l, idx):
        i = wcur["i"]
        assert wseq[i] == (kind, l, idx), (wseq[i], kind, l, idx)
        assert i < wstate["issued"] or i < NSLOT or wrel["k"] >= i - NSLOT, ("slot still busy", i, wrel["k"])
        w_issue_upto(i + 1)
        wcur["i"] += 1
        return i % NSLOT, i

    wdone = set()
    wrel = {"k": -1}

    def w_done(i):
        wdone.add(i)
        while (wrel["k"] + 1) in wdone:
            wrel["k"] += 1
        w_issue_upto(wrel["k"] + NSLOT + 1)

    for blk in cast_order[:3]:
        emit_cast(0, blk)
    w_issue_upto(NSLOT)

    def mod_ap(l, f, col):
        o = l * 48 + f * 2 + col
        return c_mod[:, o:o + 1]

    def gs_ap(l, kc, col):
        o = l * 16 + kc * 2 + col
        return c_gs[:, o:o + 1]

    def mm_group(out_ap, pairs, bank, reads):
        n = len(pairs)
        fns = []
        for k, (lt, rh) in enumerate(pairs):
            fns.append(lambda e, lt=lt, rh=rh, k=k: e.matmul(out_ap, lhsT=lt, rhs=rh, start=(k == 0), stop=(k == n - 1)))
        sch.op("pe", fns, reads=reads, writes=[b_ps[bank]])

    def mm_list(out_ap, pairs, bank, reads):
        n = len(pairs)
        lst = []
        for k, (lt, rh) in enumerate(pairs):
            lst.append(lambda k=k, lt=lt, rh=rh: sch.op(
                "pe", [lambda e: e.matmul(out_ap, lhsT=lt, rhs=rh, start=(k == 0), stop=(k == n - 1))],
                reads=reads, writes=[b_ps[bank]]))
        return lst

    sc3 = c_sc[:, :].rearrange("p (kc c) -> p kc c", c=2)

    def stage_mod_block(l, fb):
        bank = 7
        s_, wi = w_take("ada", l, fb)
        wv = wsl_f[s_]
        for j in range(2):
            mm_group(ps[bank][:, j * 2:j * 2 + 2],
                     [(wv[:, kc, j * 128:(j + 1) * 128], sc3[:, kc, :]) for kc in range(KC)],
                     bank, [b_ws[s_], b_sc])
        w_done(wi)
        f0 = fb * 2
        sch.op("dve", [lambda e: e.tensor_tensor(
            out=c_mod[:, l * 48 + f0 * 2:l * 48 + f0 * 2 + 4].rearrange("p (f c) -> p f c", c=2),
            in0=ps[bank][:, 0:4].rearrange("p (f c) -> p f c", c=2),
            in1=c_bada[:, l * 24 + f0:l * 24 + f0 + 2].unsqueeze(2).to_broadcast([128, 2, 2]),
            op=ALU.add)], reads=[b_ps[bank], b_const], writes=[b_mod[l]])

    def stage_mod_finish(l):
        sch.op("dve", [lambda e: e.scalar_tensor_tensor(
            out=c_gs[:, l * 16:(l + 1) * 16].rearrange("p (f c) -> p f c", c=2),
            in0=c_mod[:, l * 48 + 16:l * 48 + 32].rearrange("p (f c) -> p f c", c=2),
            scalar=1.0,
            in1=c_ng[:, l * 8:(l + 1) * 8].unsqueeze(2).to_broadcast([128, 8, 2]),
            op0=ALU.add, op1=ALU.mult)], reads=[b_mod[l], b_const], writes=[b_mod[l]])
        sm1, bsm1 = smalls()
        sch.op("dve", [lambda e: e.reduce_max(out=sm1[:, 0:1], in_=c_qg[:, l * 64:(l + 1) * 64], axis=AX.X,
                                              apply_absolute_value=True)], reads=[b_const], writes=[bsm1])
        sch.op("dve", [lambda e: e.reduce_max(out=sm1[:, 1:2], in_=c_kg[:, l * 64:(l + 1) * 64], axis=AX.X,
                                              apply_absolute_value=True)], reads=[b_const], writes=[bsm1])
        sch.op("dve", [lambda e: e.tensor_scalar(out=c_negB[:, l:l + 1], in0=sm1[:, 0:1], scalar1=sm1[:, 1:2],
                                                 scalar2=-8.0, op0=ALU.mult, op1=ALU.mult)],
               reads=[bsm1], writes=[b_negB[l]])


    STOP = stop
    def stage_norm(l, gi):
        t0, n, _ = groups[gi]
        bank = 6
        for kc in range(KC):
            sq, bsq = scratch()
            sch.op("act", [lambda e, sq=sq, kc=kc: e.activation(out=sq[:, 0:n], in_=xT[:, kc, t0:t0 + n], func=AF.Square)],
                   reads=[xb(gi, kc)], writes=[bsq])
            sch.op("pe", [lambda e, sq=sq, kc=kc: e.matmul(ps[bank][:, 0:n], lhsT=c_ones[:, :], rhs=sq[:, 0:n],
                                                           start=(kc == 0), stop=(kc == KC - 1))],
                   reads=[bsq, b_misc], writes=[b_ps[bank]])
        vt, bvt = scratch()
        sch.op("act", [lambda e: e.activation(out=vt[:, 0:n], in_=ps[bank][:, 0:n], func=AF.Ln, scale=1.0 / D, bias=c_eps[:, 0:1])],
               reads=[b_ps[bank], b_misc], writes=[bvt])
        sch.op("act", [lambda e: e.activation(out=rstd_all[:, t0:t0 + n], in_=vt[:, 0:n], func=AF.Exp, scale=-0.5)],
               reads=[bvt], writes=[b_rstd[gi]])

    def stage_hT(l, gi, hs, extra):
        t0, n, isctx = groups[gi]
        col = 1 if isctx else 0
        m = n + extra
        rd_extra = [b_rstd[gi + 1]] if extra else []
        for kc in range(KC):
            tmp, btmp = scratch()
            rds = [xb(gi, kc), b_rstd[gi]] + rd_extra + ([xb(gi + 1, kc)] if extra else [])
            sch.op("dve", [lambda e, tmp=tmp, kc=kc: e.tensor_tensor(out=tmp[:, 0:m], in0=xT[:, kc, t0:t0 + m],
                                                                    in1=rstd_all[:, t0:t0 + m], op=ALU.mult)],
                   reads=rds, writes=[btmp])
            sch.op("act", [lambda e, tmp=tmp, kc=kc: e.activation(out=hT[hs][:, kc, 0:m], in_=tmp[:, 0:m], func=AF.Identity,
                                                                  scale=gs_ap(l, kc, col), bias=mod_ap(l, kc, col))],
                   reads=[btmp, b_mod[l]], writes=[b_hT[hs][kc]])

    def rms_rope_a(src, bsrc, nh):
        w = nh * 64
        sq, bsq = scratch()
        sch.op("act", [lambda e: e.activation(out=sq[:, 0:w], in_=src[:, 0:w], func=AF.Square)],
               reads=[bsrc], writes=[bsq])
        return sq, bsq

    def rms_rope(l, src, bsrc, nh, gvec, tile_lat, perm, pre=None, gpvec=None):
        w = nh * 64
        sq, bsq = pre if pre is not None else rms_rope_a(src, bsrc, nh)
        ss, bss = smalls()
        sch.op("dve", [lambda e: e.reduce_sum(out=ss[:, 0:nh], in_=sq[:, 0:w].rearrange("p (h d) -> p h d", d=64), axis=AX.X)],
               reads=[bsq], writes=[bss])
        sch.op("act", [lambda e: e.activation(out=ss[:, 0:nh], in_=ss[:, 0:nh], func=AF.Ln, scale=1.0 / HD, bias=c_eps[:, 0:1])],
               reads=[bss, b_misc], writes=[bss])
        rr, brr = smalls()
        sch.op("act", [lambda e: e.activation(out=rr[:, 0:nh], in_=ss[:, 0:nh], func=AF.Exp, scale=-0.5)],
               reads=[bss], writes=[brr])
        src3 = src[:, 0:w].rearrange("p (h d) -> p h d", d=64)
        qn, bqn = scratch()
        qn3 = qn[:, 0:w].rearrange("p (h d) -> p h d", d=64)
        sch.op("dve", [lambda e: e.tensor_tensor(out=qn3, in0=src3, in1=rr[:, 0:nh].unsqueeze(2).to_broadcast([128, nh, 64]),
                                                 op=ALU.mult)], reads=[bsrc, brr], writes=[bqn])
        def out_views(dst_bf, a_in, b_in):
            if not perm:
                return [(dst_bf, a_in, b_in)]
            res = []
            for i in range(2):
                o = dst_bf[:, i * 512:(i + 1) * 512].rearrange("p (j r d) -> p r j d", j=4, r=2, d=64)
                a = a_in[:, i * 512:(i + 1) * 512].rearrange("p (r j d) -> p r j d", j=4, r=2, d=64)
                b = b_in[:, i * 512:(i + 1) * 512].rearrange("p (r j d) -> p r j d", j=4, r=2, d=64) if b_in is not None else None
                res.append((o, a, b))
            return res

        if tile_lat is None:
            dst, bdst = scratch()
            dst_bf = dst.bitcast(BF16)[:, 0:w]
            fns = []
            for o, a, _ in out_views(dst_bf, qn[:, 0:w], None):
                if perm:
                    gb = gvec.unsqueeze(1).unsqueeze(1).to_broadcast([128, 2, 4, 64])
                else:
                    o = o.rearrange("p (h d) -> p h d", d=64)
                    a = a.rearrange("p (h d) -> p h d", d=64)
                    gb = gvec.unsqueeze(1).to_broadcast([128, nh, 64])
                fns.append(lambda e, o=o, a=a, gb=gb: e.tensor_tensor(out=o, in0=a, in1=gb, op=ALU.mult))
            sch.op("dve", fns, reads=[bqn, b_const], writes=[bdst])
            return dst_bf, bdst
        ci = cgsg_ring.next()
        cg_t, bcg = cgsg[ci], b_cgsg[ci]
        sch.op("dve", [lambda e: e.tensor_tensor(out=cg_t[:, 0:64], in0=c_cos[:, tile_lat * 64:(tile_lat + 1) * 64], in1=gvec, op=ALU.mult),
                       lambda e: e.tensor_tensor(out=cg_t[:, 64:128], in0=c_sin[:, tile_lat * 64:(tile_lat + 1) * 64], in1=gpvec, op=ALU.mult)],
               reads=[b_const], writes=[bcg])
        cosv = cg_t[:, 0:64]
        sinv = cg_t[:, 64:128]
        t1, bt1 = scratch()
        t13 = t1[:, 0:w].rearrange("p (h d) -> p h d", d=64)
        sch.op("dve", [lambda e: e.tensor_tensor(out=t13, in0=qn3, in1=cosv.unsqueeze(1).to_broadcast([128, nh, 64]),
                                                 op=ALU.mult)], reads=[bqn, bcg], writes=[bt1])
        t2, bt2 = scratch()
        qn5 = qn[:, 0:w].rearrange("p (h a b d) -> p h a b d", a=2, b=2, d=16)
        t25 = t2[:, 0:w].rearrange("p (h a b d) -> p h a b d", a=2, b=2, d=16)
        sin4 = sinv.rearrange("p (a b d) -> p a b d", a=2, b=2, d=16)
        fns = []
        for hh in range(2):
            for bb in range(2):
                fns.append(lambda e, hh=hh, bb=bb: e.tensor_tensor(
                    out=t25[:, :, hh, bb, :], in0=qn5[:, :, hh, 1 - bb, :],
                    in1=sin4[:, hh, bb, :].unsqueeze(1).to_broadcast([128, nh, 16]), op=ALU.mult))
        sch.op("dve", fns, reads=[bqn, bcg], writes=[bt2])
        dst, bdst = scratch()
        dst_bf = dst.bitcast(BF16)[:, 0:w]
        fns = []
        for o, a, b in out_views(dst_bf, t1[:, 0:w], t2[:, 0:w]):
            fns.append(lambda e, o=o, a=a, b=b: e.tensor_tensor(out=o, in0=a, in1=b, op=ALU.add))
        sch.op("dve", fns, reads=[bt1, bt2], writes=[bdst])
        return dst_bf, bdst

    def pipeline(phases_per_tile):
        nt = len(phases_per_tile)
        phases_per_tile[0][0]()
        phases_per_tile[0][3]()
        phases_per_tile[0][1]()
        for t in range(nt):
            if t + 1 < nt:
                phases_per_tile[t + 1][0]()
                phases_per_tile[t + 1][3]()
            phases_per_tile[t][2]()
            if t + 1 < nt:
                phases_per_tile[t + 1][1]()

    def stage_kv(l, gi, hs, slot):
        t0, n, isctx = groups[gi]
        tiles = []
        for ti in range(n // 128):
            kt = t0 // 128 + ti
            st = {}

            def f1(ti=ti, kt=kt, st=st):
                bank = 4 + (kt % 2)
                mm_group(ps[bank][:, :],
                         [(hT[hs][:, kc, ti * 128:(ti + 1) * 128], wsl[slot][:, kc, :]) for kc in range(KC)],
                         bank, b_hT[hs] + [b_ws[slot]])
                kv, bkv = scratch()
                st["kv"], st["bkv"] = kv, bkv
                sch.op("act", [lambda e, kv=kv, bank=bank: e.activation(out=kv[:, 0:256], in_=ps[bank][:, 0:256], func=AF.Copy)],
                       reads=[b_ps[bank]], writes=[bkv])
                sch.op("act", [lambda e, bank=bank, kt=kt: e.activation(
                    out=Vt5[:, kt, :, 0:3:2, :],
                    in_=ps[bank][:, 256:512].rearrange("p (m r d) -> p m r d", m=2, r=2, d=64), func=AF.Copy)],
                    reads=[b_ps[bank]], writes=[b_V[kt]])

            def f2(ti=ti, kt=kt, st=st):
                kv, bkv = st["kv"], st["bkv"]
                st["kr"], st["bkr"] = rms_rope(l, kv, bkv, NKV, c_kg[:, l * 64:(l + 1) * 64],
                                               None if isctx else (kt - CTX // 128), False, pre=st["pre"],
                                               gpvec=c_kgp[:, l * 64:(l + 1) * 64])

            def f2a(ti=ti, kt=kt, st=st):
                st["pre"] = rms_rope_a(st["kv"], st["bkv"], NKV)

            def f3(ti=ti, kt=kt, st=st):
                kr_bf, bkr = st["kr"], st["bkr"]
                tb = 7
                fns = []
                for j in range(2):
                    fns.append(lambda e, j=j, kr_bf=kr_bf: e.transpose(out=ps_bf[tb][:, j * 128:(j + 1) * 128],
                                                                       in_=kr_bf[:, j * 128:(j + 1) * 128], identity=c_ident[:, :]))
                sch.op("pe", fns, reads=[bkr, b_ident], writes=[b_ps[tb]])
                fns = []
                for r in range(2):
                    fns.append(lambda e, kt=kt, r=r: e.activation(
                        out=kT[r * 64:(r + 1) * 64, r:4:2, kt * 128:(kt + 1) * 128],
                        in_=ps_bf[tb][r * 64:(r + 1) * 64, 0:256].rearrange("p (j t) -> p j t", j=2), func=AF.Copy))
                sch.op("act", fns, reads=[b_ps[tb]], writes=[b_kT[kt]])

            tiles.append((f1, f2, f3, f2a))
        pipeline(tiles)

    def stage_q(l, gi, hs, s0, s1, qb_i):
        t0, n, isctx = groups[gi]
        qT = gbuf[qb_i]
        tiles = []
        for ti in range(n // 128):
            st = {}

            def f1(ti=ti, st=st):
                q_sb, bq = scratch()
                st["q"], st["bq"] = q_sb, bq
                for blk, sl in enumerate((s0, s1)):
                    bank = 4 + blk
                    mm_group(ps[bank][:, :],
                             [(hT[hs][:, kc, ti * 128:(ti + 1) * 128], wsl[sl][:, kc, :]) for kc in range(KC)],
                             bank, b_hT[hs] + [b_ws[sl]])
                    sch.op("act", [lambda e, blk=blk, bank=bank, q_sb=q_sb: e.activation(out=q_sb[:, blk * 512:(blk + 1) * 512],
                                                                                         in_=ps[bank][:, :], func=AF.Copy)],
                           reads=[b_ps[bank]], writes=[bq])

            def f2(ti=ti, st=st):
                tile_lat = None if isctx else (t0 - CTX) // 128 + ti
                st["qr"], st["bqr"] = rms_rope(l, st["q"], st["bq"], NH, c_qg[:, l * 64:(l + 1) * 64], tile_lat, True,
                                               pre=st["pre"], gpvec=c_qgp[:, l * 64:(l + 1) * 64])

            def f2a(ti=ti, st=st):
                st["pre"] = rms_rope_a(st["q"], st["bq"], NH)

            def f3(ti=ti, st=st):
                qr_bf, bqr = st["qr"], st["bqr"]
                tb = 7
                fns = []
                for c in range(8):
                    fns.append(lambda e, c=c, qr_bf=qr_bf: e.transpose(out=ps_bf[tb][:, c * 128:(c + 1) * 128],
                                                                       in_=qr_bf[:, c * 128:(c + 1) * 128], identity=c_ident[:, :]))
                sch.op("pe", fns, reads=[bqr, b_ident], writes=[b_ps[tb]])
                sch.op("act", [lambda e, ti=ti: e.activation(out=qT[:, :, ti * 128:(ti + 1) * 128],
                                                             in_=ps_bf[tb][:, :].rearrange("p (c t) -> p c t", c=8),
                                                             func=AF.Copy)], reads=[b_ps[tb]], writes=b_g[qb_i])

            tiles.append((f1, f2, f3, f2a))
        pipeline(tiles)

    def stage_att(l, gi, q_i, a_i, feeder=None):
        t0, n, isctx = groups[gi]
        qT = gbuf[q_i]
        ga = gbuf[a_i]
        kts = [0, 1] if isctx else list(range(18))
        pairs = [(kts[2 * p], kts[2 * p + 1]) for p in range(len(kts) // 2)]
        npair = len(pairs)
        iters = [(g, qb) for g in range(NKV) for qb in range(n // 128)]
        steps = [(it, p) for it in range(len(iters)) for p in range(npair)]

        def emit_S(k):
            it, p = steps[k]
            g, qb = iters[it]
            i = g // 2
            sp = k % 2
            fns = []
            for h, kt in enumerate(pairs[p]):
                fns.append(lambda e, kt=kt, h=h, sp=sp, g=g, i=i, qb=qb: e.matmul(
                    psS[sp][:, h * 512:(h + 1) * 512], lhsT=kT[:, g, kt * 128:(kt + 1) * 128],
                    rhs=qT[:, 4 * i:4 * i + 4, qb * 128:(qb + 1) * 128], start=True, stop=True))
            sch.op("pe", fns, reads=[b_kT[pairs[p][0]], b_kT[pairs[p][1]]] + b_g[q_i], writes=[b_ps[2 * sp], b_ps[2 * sp + 1]])

        emit_S(0)
        if len(steps) > 1:
            emit_S(1)
        for k, (it, p) in enumerate(steps):
            g, qb = iters[it]
            if feeder and p == 0:
                feeder.start(it + 1)
            r = g % 2
            voff = (g // 2) * 192 + r * 64
            nump = r * 64
            denp = (1 - r) * 64
            ob = 4 + (it % 2)
            sp = k % 2
            pt, bpt = scratch()
            pt_bf = pt.bitcast(BF16)
            sch.op("act", [lambda e, pt_bf=pt_bf, sp=sp: e.activation(
                out=pt_bf[:, 0:1024], in_=psS[sp][:, 0:1024], func=AF.Exp, scale=HD ** -0.5, bias=c_negB[:, l:l + 1])],
                reads=[b_ps[2 * sp], b_ps[2 * sp + 1], b_negB[l]], writes=[bpt])
            if k + 2 < len(steps):
                emit_S(k + 2)
            fns = []
            for h, kt in enumerate(pairs[p]):
                first = (p == 0 and h == 0)
                lastm = (p == npair - 1 and h == 1)
                fns.append(lambda e, kt=kt, h=h, pt_bf=pt_bf, ob=ob, voff=voff, first=first, lastm=lastm: e.matmul(
                    ps[ob][:, :], lhsT=Vt[:, kt, voff:voff + 128], rhs=pt_bf[:, h * 512:(h + 1) * 512],
                    start=first, stop=lastm))
            sch.op("pe", fns, reads=[bpt, b_V[pairs[p][0]], b_V[pairs[p][1]]], writes=[b_ps[ob]])
            if feeder:
                feeder.pump(2)
            if p == npair - 1:
                rd, brd = rd_t, b_rd
                sch.op("dve", [lambda e, ob=ob, rd=rd, nump=nump, denp=denp: e.reciprocal(
                    out=rd[nump:nump + 64, 0:512], in_=ps[ob][denp:denp + 64, :])], reads=[b_ps[ob]], writes=[brd])
                o4 = ps[ob][nump:nump + 64, :].rearrange("p (j q) -> p j q", j=4)
                r4 = rd[nump:nump + 64, 0:512].rearrange("p (j q) -> p j q", j=4)
                fns = []
                for par in range(2):
                    fns.append(lambda e, par=par, o4=o4, r4=r4, g=g, qb=qb: e.tensor_tensor(
                        out=ga[par * 64:(par + 1) * 64, 2 * g:2 * g + 2, qb * 128:(qb + 1) * 128],
                        in0=o4[:, par:4:2, :], in1=r4[:, par:4:2, :], op=ALU.mult))
                sch.op("dve", fns, reads=[b_ps[ob], brd], writes=[b_g[a_i][2 * g], b_g[a_i][2 * g + 1]])
                if feeder:
                    feeder.finish()

    def stage_za(l, gi, hs, s0, s1, a_i):
        t0, n, isctx = groups[gi]
        ga = gbuf[a_i]
        for c in range(8):
            sl = (s0, s1)[c // 4]
            cols = (c % 4) * 128
            bank = 4 + (c % 4)
            mm_group(ps[bank][:, 0:n], [(wsl[sl][:, kc, cols:cols + 128], hT[hs][:, kc, 0:n]) for kc in range(KC)],
                     bank, b_hT[hs] + [b_ws[sl]])
            sz, bsz = scratch()
            sch.op("act", [lambda e, sz=sz, bank=bank: e.activation(out=sz[:, 0:n], in_=ps[bank][:, 0:n], func=AF.Silu)],
                   reads=[b_ps[bank]], writes=[bsz])
            sch.op("dve", [lambda e, sz=sz, c=c: e.tensor_tensor(out=ga[:, c, 0:n], in0=ga[:, c, 0:n], in1=sz[:, 0:n],
                                                                 op=ALU.mult)],
                   reads=[bsz, b_g[a_i][c]], writes=[b_g[a_i][c]])

    def conv_half_a(l, gi, hs, c, slot, bk=(6, 7)):
        t0, n, isctx = groups[gi]
        ot = 1 - hs
        u = hT_f[ot][:, 0:514]
        y = hT_f[ot][:, 514:1026]
        bu = b_hT[ot]
        first = (gi == 1)
        lastg = (gi == NG - 1)
        if isctx:
            off, m, a = 0, n, 1
        else:
            off, m, a = 1, (n - 1 if lastg else n), 2
        mms = []
        for k, bank in ((0, bk[0]), (1, bk[1])):
            mms += mm_list(ps[bank][:, 0:m], [(wsl[slot][:, kc, k * 128:(k + 1) * 128], hT[hs][:, kc, off:off + m]) for kc in range(KC)],
                           bank, b_hT[hs] + [b_ws[slot]])
        return mms, (lambda: conv_evac_a(l, gi, hs, c, slot, bk))

    def conv_evac_a(l, gi, hs, c, slot, bk=(6, 7)):
        t0, n, isctx = groups[gi]
        ot = 1 - hs
        u = hT_f[ot][:, 0:514]
        y = hT_f[ot][:, 514:1026]
        bu = b_hT[ot]
        first = (gi == 1)
        lastg = (gi == NG - 1)
        if isctx:
            off, m, a = 0, n, 1
        else:
            off, m, a = 1, (n - 1 if lastg else n), 2
        xs, bxs = scratch()
        sch.op("act", [lambda e: e.activation(out=xs[:, 0:m], in_=ps[bk[0]][:, 0:m], func=AF.Copy)], reads=[b_ps[bk[0]]], writes=[bxs])
        sch.op("dve", [lambda e: e.tensor_tensor(out=u[:, a:a + m], in0=ps[bk[1]][:, 0:m], in1=xs[:, 0:m], op=ALU.mult)],
               reads=[b_ps[bk[1]], bxs], writes=bu)
        if isctx:
            sch.op("dve", [lambda e: e.memset(u[:, 0:1], 0.0), lambda e: e.memset(u[:, n + 1:n + 2], 0.0)], writes=bu)
        else:
            if first:
                for k, bank in ((0, bk[0]), (1, bk[1])):
                    mm_group(ps[bank][:, 0:1], [(wsl[slot][:, kc, k * 128:(k + 1) * 128], hT[hs][:, kc, 0:1]) for kc in range(KC)],
                             bank, b_hT[hs] + [b_ws[slot]])
                hx, bhx = smalls()
                sch.op("act", [lambda e: e.activation(out=hx[:, 0:1], in_=ps[bk[0]][:, 0:1], func=AF.Copy)], reads=[b_ps[bk[0]]], writes=[bhx])
                sch.op("dve", [lambda e: e.tensor_tensor(out=u[:, 1:2], in0=ps[bk[1]][:, 0:1], in1=hx[:, 0:1], op=ALU.mult)],
                       reads=[b_ps[bk[1]], bhx], writes=bu)
                sch.op("dve", [lambda e: e.memset(u[:, 0:1], 0.0)], writes=bu)
            else:
                sch.op("dve", [lambda e: e.tensor_copy(out=u[:, 0:2], in_=uprev[:, 2 * c:2 * c + 2])], reads=[b_uprev[c]], writes=bu)
            if lastg:
                sch.op("dve", [lambda e: e.memset(u[:, n + 1:n + 2], 0.0)], writes=bu)
            else:
                sch.op("dve", [lambda e: e.tensor_copy(out=uprev[:, 2 * c:2 * c + 2], in_=u[:, n:n + 2])], reads=bu, writes=[b_uprev[c]])
        cw = lambda k: c_convw[:, l * 24 + k * 8 + c:l * 24 + k * 8 + c + 1]
        sch.op("dve", [lambda e: e.tensor_scalar(out=y[:, 0:n], in0=u[:, 0:n], scalar1=cw(0), scalar2=None, op0=ALU.mult)],
               reads=bu + [b_const], writes=bu)
        sch.op("dve", [lambda e: e.scalar_tensor_tensor(out=y[:, 0:n], in0=u[:, 1:n + 1], scalar=cw(1), in1=y[:, 0:n],
                                                        op0=ALU.mult, op1=ALU.add)], reads=bu + [b_const], writes=bu)
        sch.op("dve", [lambda e: e.scalar_tensor_tensor(out=y[:, 0:n], in0=u[:, 2:n + 2], scalar=cw(2), in1=y[:, 0:n],
                                                        op0=ALU.mult, op1=ALU.add)], reads=bu + [b_const], writes=bu)

    def conv_half_b(l, gi, hs, c, slot, y_i, bk=(6, 7)):
        t0, n, isctx = groups[gi]
        ot = 1 - hs
        y = hT_f[ot][:, 514:1026]
        bu = b_hT[ot]
        yz = gbuf[y_i]
        mms = []
        for k, bank in ((2, bk[0]), (3, bk[1])):
            mms += mm_list(ps[bank][:, 0:n], [(wsl[slot][:, kc, k * 128:(k + 1) * 128], hT[hs][:, kc, 0:n]) for kc in range(KC)],
                           bank, b_hT[hs] + [b_ws[slot]])
        return mms, (lambda: conv_evac_b(l, gi, hs, c, slot, y_i, bk))

    def conv_evac_b(l, gi, hs, c, slot, y_i, bk=(6, 7)):
        t0, n, isctx = groups[gi]
        ot = 1 - hs
        y = hT_f[ot][:, 514:1026]
        bu = b_hT[ot]
        yz = gbuf[y_i]
        cb = c_convb[:, l * 8 + c:l * 8 + c + 1]
        sch.op("dve", [lambda e: e.scalar_tensor_tensor(out=y[:, 0:n], in0=y[:, 0:n], scalar=cb, in1=ps[bk[0]][:, 0:n],
                                                        op0=ALU.add, op1=ALU.mult)], reads=bu + [b_ps[bk[0]], b_const], writes=bu)
        tz, btz = scratch()
        sch.op("act", [lambda e: e.activation(out=tz[:, 0:n], in_=ps[bk[1]][:, 0:n], func=AF.Tanh, scale=0.5)],
               reads=[b_ps[bk[1]]], writes=[btz])
        sch.op("dve", [lambda e: e.scalar_tensor_tensor(out=tz[:, 0:n], in0=tz[:, 0:n], scalar=1.0, in1=ps[bk[1]][:, 0:n],
                                                        op0=ALU.add, op1=ALU.mult)], reads=[btz, b_ps[bk[1]]], writes=[btz])
        sch.op("dve", [lambda e: e.scalar_tensor_tensor(out=yz[:, c, 0:n], in0=y[:, 0:n], scalar=0.5, in1=tz[:, 0:n],
                                                        op0=ALU.mult, op1=ALU.mult)], reads=bu + [btz], writes=[b_g[y_i][c]])

    def stage_merge(l, gi, hs, j, slot, a_i, y_i, m_i):
        t0, n, isctx = groups[gi]
        pb = (j % 2) * 4
        srcs = [(gbuf[a_i], b_g[a_i]), (gbuf[y_i], b_g[y_i]), (hT[hs], b_hT[hs]), (hT[hs], b_hT[hs])]
        for k in range(4):
            bank = pb + k
            src, bsrc = srcs[k]
            mm_group(ps[bank][:, 0:n], [(wsl[slot][:, kc, k * 128:(k + 1) * 128], src[:, kc, 0:n]) for kc in range(KC)],
                     bank, bsrc + [b_ws[slot]])
        s0, bs0 = scratch()
        s1, bs1 = scratch()
        for k, (sx, bsx) in enumerate(((s0, bs0), (s1, bs1))):
            bg = c_bgate[:, l * 16 + k * 8 + j:l * 16 + k * 8 + j + 1]
            sch.op("act", [lambda e, sx=sx, k=k, bg=bg: e.activation(out=sx[:, 0:n], in_=ps[pb + 2 + k][:, 0:n],
                                                                     func=AF.Sigmoid, bias=bg)],
                   reads=[b_ps[pb + 2 + k], b_const], writes=[bsx])
        sch.op("dve", [lambda e: e.tensor_tensor(out=s0[:, 0:n], in0=s0[:, 0:n], in1=ps[pb][:, 0:n], op=ALU.mult)],
               reads=[bs0, b_ps[pb]], writes=[bs0])
        sch.op("dve", [lambda e: e.tensor_tensor(out=s1[:, 0:n], in0=s1[:, 0:n], in1=ps[pb + 1][:, 0:n], op=ALU.mult)],
               reads=[bs1, b_ps[pb + 1]], writes=[bs1])
        sch.op("dve", [lambda e: e.tensor_tensor(out=gbuf[m_i][:, j, 0:n], in0=s0[:, 0:n], in1=s1[:, 0:n], op=ALU.add)],
               reads=[bs0, bs1], writes=[b_g[m_i][j]])

    def stage_out(l, gi, s0, s1, m_i):
        t0, n, isctx = groups[gi]
        col = 1 if isctx else 0
        mg = gbuf[m_i]
        for j in range(8):
            sl = (s0, s1)[j // 4]
            cols = (j % 4) * 128
            bank = j % 4
            mm_group(ps[bank][:, 0:n], [(wsl[sl][:, kc, cols:cols + 128], mg[:, kc, 0:n]) for kc in range(KC)],
                     bank, b_g[m_i] + [b_ws[sl]])
            sch.op("dve", [lambda e, j=j, bank=bank: e.scalar_tensor_tensor(
                out=xT[:, j, t0:t0 + n], in0=ps[bank][:, 0:n], scalar=mod_ap(l, 16 + j, col), in1=xT[:, j, t0:t0 + n],
                op0=ALU.mult, op1=ALU.add)], reads=[b_ps[bank], xb(gi, j), b_mod[l]], writes=[xb(gi, j)])

    outT_dr = d_out.rearrange("(kc p) t -> p kc t", p=128)

    def store_group(gi):
        t0, n, _ = groups[gi]
        sch.dma("sp", "out", (lambda e: e.dma_start(out=outT_dr[:, :, t0:t0 + n], in_=xT[:, :, t0:t0 + n])),
                reads=xbs(gi), writes=[b_out])

    norm_done = set()
    for gi in range(NG):
        stage_norm(0, gi)
        norm_done.add((0, gi))
    for fb in range(12):
        stage_mod_block(0, fb)
    stage_mod_finish(0)

    hring = Ring([0, 1])
    pre_hT = {}
    for l in range(L):
        last = layers[l] == DEPTH - 1
        final = (l == L - 1)
        skv, wkv = w_take("cat", l, 0)
        sch.mark("L%d prepass" % l)
        for gi in range(NG):
            hs = hring.next()
            if (l, gi) not in norm_done:
                stage_norm(l, gi)
            stage_hT(l, gi, hs, 0)
            stage_kv(l, gi, hs, skv)
        w_done(wkv)
        if l == 0:
            for blk in cast_order[3:]:
                emit_cast(0, blk, after=[b_kT[17]])
        if STOP == "pre":
            break
        for gi in GORDER:
            t0, n, isctx = groups[gi]
            if isctx and last:
                if final:
                    store_group(gi)
                continue
            has_right = (not isctx) and gi < NG - 1
            has_left = (not isctx) and gi > 1
            sch.mark("L%d g%d hT+q" % (l, gi))
            if pre_hT.get((l, gi)) is None:
                hs = hring.next()
                stage_hT(l, gi, hs, 1 if has_right else 0)
            else:
                hs = pre_hT[(l, gi)]
            q_i, a_i, y_i = 0, 1, 2
            m_i = 0
            sq0, wq0 = w_take("cat", l, 1)
            sq1, wq1 = w_take("cat", l, 2)
            stage_q(l, gi, hs, sq0, sq1, q_i)
            w_done(wq0)
            w_done(wq1)
            if STOP == "q":
                break
            if l + 1 < L and gi != 1:
                per = (NBLK + (NG - 1) - 1) // (NG - 1)
                pos_c = GORDER.index(gi) - 1
                for blk in cast_order[pos_c * per:(pos_c + 1) * per]:
                    emit_cast(l + 1, blk)
            sch.mark("L%d g%d att" % (l, gi))
            ada = {}
            if l + 1 < L and isctx:
                ada = {hk: hk - 1 for hk in range(1, 13)}

            class Feeder:
                def __init__(self):
                    self.mms = []
                    self.evac = None
                    self.slot = None
                    self.wi = None

                def start(self, hk, l=l, gi=gi, hs=hs, y_i=y_i, ada=ada, isctx=isctx):
                    if hk in ada:
                        stage_mod_block(l + 1, ada[hk])
                    c = (hk - 1) // 2
                    bk = ((6, 5) if hk % 2 == 1 else (4, 3)) if isctx else (6, 7)
                    if hk % 2 == 1:
                        self.slot, self.wi = w_take("cat", l, 5 + c)
                        self.mms, self.evac = conv_half_a(l, gi, hs, c, self.slot, bk)
                        self.release = None
                    else:
                        self.mms, self.evac = conv_half_b(l, gi, hs, c, self.slot, y_i, bk)
                        self.release = self.wi

                def pump(self, k):
                    for _ in range(k):
                        if self.mms:
                            self.mms.pop(0)()
                    if not self.mms and self.evac:
                        ev, self.evac = self.evac, None
                        if self.release is not None:
                            w_done(self.release)
                        ev()

                def finish(self):
                    self.pump(10 ** 6)

            fd = Feeder()
            if isctx:
                stage_att(l, gi, q_i, a_i, None)
                for hk in range(1, 17):
                    fd.start(hk)
                    fd.finish()
            else:
                stage_att(l, gi, q_i, a_i, fd)
            if l + 1 < L and isctx:
                stage_mod_finish(l + 1)
            sch.mark("L%d g%d za" % (l, gi))
            sz0, wz0 = w_take("cat", l, 3)
            sz1, wz1 = w_take("cat", l, 4)
            stage_za(l, gi, hs, sz0, sz1, a_i)
            w_done(wz0)
            w_done(wz1)
            sch.mark("L%d g%d merge" % (l, gi))
            for j in range(8):
                sm_, wm_ = w_take("cat", l, 13 + j)
                stage_merge(l, gi, hs, j, sm_, a_i, y_i, m_i)
                w_done(wm_)
            order_l = [g_ for g_ in GORDER if not (groups[g_][2] and last)]
            pos = order_l.index(gi)
            if pos + 1 < len(order_l):
                gn = order_l[pos + 1]
                hs_n = hring.next()
                assert hs_n == 1 - hs
                stage_hT(l, gn, hs_n, 1 if ((not groups[gn][2]) and gn < NG - 1) else 0)
                pre_hT[(l, gn)] = hs_n
            sch.mark("L%d g%d out" % (l, gi))
            so0, wo0 = w_take("cat", l, 21)
            so1, wo1 = w_take("cat", l, 22)
            stage_out(l, gi, so0, so1, m_i)
            w_done(wo0)
            w_done(wo1)
            if l + 1 < L:
                stage_norm(l + 1, gi)
                norm_done.add((l + 1, gi))
            if final:
                store_group(gi)
    if STOP is None:
        assert wcur["i"] == len(wseq), (wcur["i"], len(wseq))
    else:
        for gi in range(NG):
            store_group(gi)
    sch.wait_all("sp", [b_out.w])

    sch.mark("end")
    with nc.Block() as block:
        sch.replay(block)
    nc._stage_marks = sch.marks
    return nc


def _rope_tables():
    t = np.arange(S)
    row = (t // 64).astype(np.float32)
    col = (t % 64).astype(np.float32)
    half = HD // 2
    inv = (np.float32(10000.0) ** (-np.arange(0, half, 2, dtype=np.float32) / np.float32(half))).astype(np.float32)
    ang_r = row[:, None] * inv[None, :]
    ang_c = col[:, None] * inv[None, :]
    ang = np.concatenate([ang_r, ang_r, ang_c, ang_c], axis=-1).astype(np.float32)
    cos = np.cos(ang).astype(np.float32)
    sin = np.sin(ang).astype(np.float32)
    sign = np.concatenate([-np.ones(16), np.ones(16), -np.ones(16), np.ones(16)]).astype(np.float32)
    sinS = sin * sign[None, :]

    def lay(a):
        return np.ascontiguousarray(a.reshape(16, 128, 64).transpose(1, 0, 2).reshape(128, 16 * 64))

    return lay(cos), lay(sinS)


def _pvec(v):
    v = np.asarray(v, dtype=np.float32)
    lead = v.shape[:-1]
    n = v.shape[-1] // 128
    a = v.reshape(lead + (n, 128))
    a = np.moveaxis(a, -1, 0)
    return np.ascontiguousarray(a.reshape(128, -1))


def _wcat(w_in, w_attn_br, w_conv_br, w_out):
    blocks = []
    q = w_in[:, 0:1024]
    k = w_in[:, 1024:1280]
    v = w_in[:, 1280:1536]
    za = w_in[:, 1536:2560]
    xc = w_in[:, 2560:3584]
    bc = w_in[:, 3584:4608]
    cc = w_in[:, 4608:5632]
    zc = w_in[:, 5632:6656]
    gl0 = w_in[:, 6656:7680]
    gl1 = w_in[:, 7680:8704]
    blocks.append(np.concatenate([k, v], axis=1))
    blocks += [q[:, 0:512], q[:, 512:1024], za[:, 0:512], za[:, 512:1024]]
    for c in range(8):
        s = slice(c * 128, (c + 1) * 128)
        blocks.append(np.concatenate([xc[:, s], cc[:, s], bc[:, s], zc[:, s]], axis=1))
    for j in range(8):
        s = slice(j * 128, (j + 1) * 128)
        blocks.append(np.concatenate([w_attn_br[:, s], w_conv_br[:, s], gl0[:, s], gl1[:, s]], axis=1))
    blocks += [w_out[:, 0:512], w_out[:, 512:1024]]
    return np.ascontiguousarray(np.stack(blocks, axis=0), dtype=np.float32)


_SWAP = np.concatenate([np.arange(16, 32), np.arange(0, 16), np.arange(48, 64), np.arange(32, 48)])

_PROG_CACHE = {}


def _get_prog(layers):
    key = tuple(layers)
    if key not in _PROG_CACHE:
        _PROG_CACHE[key] = build_program(list(layers))
    return _PROG_CACHE[key]


def _run_layers(layers, xall_T, inputs):
    f = lambda a: np.asarray(a, dtype=np.float32)
    L = len(layers)
    li = list(layers)
    shared = {
        "wada": np.ascontiguousarray(f(inputs["w_ada"])[li]),
        "bada": np.concatenate([_pvec(f(inputs["b_ada"])[l]) for l in li], axis=1),
        "ng": np.concatenate([_pvec(f(inputs["norm_g"])[l]) for l in li], axis=1),
        "wcat": np.stack([_wcat(f(inputs["w_in"])[l], f(inputs["w_attn_br"])[l], f(inputs["w_conv_br"])[l],
                                f(inputs["w_out"])[l]) for l in li], axis=0),
        "qg": np.concatenate([np.tile(f(inputs["q_norm_g"])[l][None, :], (128, 1)) for l in li], axis=1),
        "kg": np.concatenate([np.tile(f(inputs["k_norm_g"])[l][None, :], (128, 1)) for l in li], axis=1),
        "qgp": np.concatenate([np.tile(f(inputs["q_norm_g"])[l][_SWAP][None, :], (128, 1)) for l in li], axis=1),
        "kgp": np.concatenate([np.tile(f(inputs["k_norm_g"])[l][_SWAP][None, :], (128, 1)) for l in li], axis=1),
        "convw": np.concatenate([_pvec(f(inputs["conv_w"])[l]) for l in li], axis=1),
        "convb": np.concatenate([_pvec(f(inputs["conv_b"])[l]) for l in li], axis=1),
        "bgate": np.concatenate([_pvec(f(inputs["b_gate"])[l]) for l in li], axis=1),
        "ident": np.eye(128, dtype=np.float32),
    }
    cosT, sinT = _rope_tables()
    shared["cosT"] = cosT
    shared["sinT"] = sinT
    shared = {k: np.ascontiguousarray(v, dtype=np.float32) for k, v in shared.items()}
    c = f(inputs["c"])
    c_ctx = f(inputs["c_ctx"])
    in_maps = []
    for b in range(N_CORES):
        cv = np.stack([c[b].reshape(KC, 128).T, c_ctx.reshape(KC, 128).T], axis=-1)
        m = dict(shared)
        m["xT"] = np.ascontiguousarray(xall_T[b], dtype=np.float32)
        m["cvec"] = np.ascontiguousarray(cv.reshape(128, KC * 2), dtype=np.float32)
        in_maps.append(m)
    nc = _get_prog(layers)
    res = run_bass_kernel_spmd(nc, in_maps, core_ids=list(range(N_CORES)))
    return [np.asarray(r["outT"]) for r in res.results]


FUSED = True


def kernel(x, c, ctx, c_ctx, norm_g, w_ada, b_ada, w_in, q_norm_g, k_norm_g, conv_w, conv_b,
           w_attn_br, w_conv_br, b_gate, w_out):
    inputs = dict(x=x, c=c, ctx=ctx, c_ctx=c_ctx, norm_g=norm_g, w_ada=w_ada, b_ada=b_ada, w_in=w_in,
                  q_norm_g=q_norm_g, k_norm_g=k_norm_g, conv_w=conv_w, conv_b=conv_b, w_attn_br=w_attn_br,
                  w_conv_br=w_conv_br, b_gate=b_gate, w_out=w_out)
    x = np.asarray(x, dtype=np.float32)
    ctx = np.asarray(ctx, dtype=np.float32)
    xall_T = [np.ascontiguousarray(np.concatenate([ctx[b], x[b]], axis=0).T) for b in range(N_CORES)]
    if FUSED:
        outs = _run_layers([0, 1], xall_T, inputs)
    else:
        mid = _run_layers([0], xall_T, inputs)
        outs = _run_layers([1], mid, inputs)
    out = np.stack([o[:, CTX:].T for o in outs], axis=0)
    return np.ascontiguousarray(out, dtype=np.float32)
```
